# Optimizing a Trainium2 kernel written in Bass

```python
import math
import jax
import jax.numpy as jnp
from jax import lax
import numpy as np

D_MODEL = 1024
BATCH = 16
SEQ = 256
DEPTH = 4
DEC_BATCH = 4
DEC_SEQ = 1024
PAST_LEN = 256

GRID_W = 64
GROUP_W = 512
D_MIX = 3 * GROUP_W
MLA_HEADS = 4
MLA_NOPE = 128
MLA_ROPE = 64
MLA_V = 128
MLA_Q_RANK = 384
MLA_KV_RANK = 256
DIFF_HEADS = 4
DIFF_D = 64
ML_HEADS = 4
ML_DK = 128
ML_DV = 128
ML_CHUNK = 64
CONV_K = 3
FORGET_BIAS = 3.0
ROPE_THETA = 10000.0
NORM_EPS = 1e-6
ATTN_BLOCK = 128
N_ML_GATES = 2 * 2 * ML_HEADS
IN_SIZES = (MLA_Q_RANK, MLA_KV_RANK, MLA_ROPE, GROUP_W,
            GROUP_W, GROUP_W, GROUP_W, GROUP_W,
            GROUP_W, GROUP_W, GROUP_W, GROUP_W, GROUP_W,
            N_ML_GATES)
N_IN = int(sum(IN_SIZES))
IN_SPLITS = tuple(int(s) for s in np.cumsum(IN_SIZES)[:-1])
MLA_SCALE = (MLA_NOPE + MLA_ROPE) ** -0.5
DIFF_SCALE = DIFF_D ** -0.5

kernel_name = 'hybrid_mla_diff_mlstm_dit_step'


def rmsnorm(x, g):
    xf = x.astype(jnp.float32)
    y = xf * lax.rsqrt(jnp.mean(xf * xf, axis=-1, keepdims=True) + NORM_EPS)
    return y.astype(x.dtype) * g.astype(x.dtype)


def axial_rope_tables(n_rows, rot_dim):
    n_freq = rot_dim // 4
    inv = ROPE_THETA ** (-jnp.arange(n_freq, dtype=jnp.float32) / n_freq)
    row = jnp.repeat(jnp.arange(n_rows, dtype=jnp.float32), GRID_W)
    col = jnp.tile(jnp.arange(GRID_W, dtype=jnp.float32), n_rows)
    ang = jnp.concatenate([row[:, None] * inv, col[:, None] * inv], axis=-1)
    return jnp.cos(ang), jnp.sin(ang)


def apply_rope(x, cos, sin):
    half = x.shape[-1] // 2
    c = cos[:, None, :].astype(x.dtype)
    s = sin[:, None, :].astype(x.dtype)
    x1, x2 = x[..., :half], x[..., half:]
    return jnp.concatenate([x1 * c - x2 * s, x1 * s + x2 * c], axis=-1)


def rope_sub(x, cos, sin):
    B, T, H, E = x.shape
    return apply_rope(x.reshape(B, T, 2 * H, DIFF_D), cos, sin).reshape(B, T, H, E)


def attention(q, k, v, scale):
    B, Tq, H, dq = q.shape
    nb = Tq // ATTN_BLOCK
    qb = jnp.moveaxis(q.reshape(B, nb, ATTN_BLOCK, H, dq), 1, 0)

    def one_block(q_blk):
        s = jnp.einsum('bqhd,bkhd->bhqk', q_blk, k).astype(jnp.float32) * scale
        p = jax.nn.softmax(s, axis=-1).astype(v.dtype)
        return jnp.einsum('bhqk,bkhd->bqhd', p, v)

    o = lax.map(one_block, qb)
    return jnp.moveaxis(o, 0, 1).reshape(B, Tq, H, v.shape[-1])


def short_conv(u, w):
    ch = u.shape[-1]
    return lax.conv_general_dilated(
        u, w[:, None, :].astype(u.dtype), window_strides=(1,),
        padding=[(CONV_K // 2, CONV_K // 2)],
        dimension_numbers=('NWC', 'WIO', 'NWC'), feature_group_count=ch)


def mlstm_scan(q, k, v, ig, lf, C0, n0, m0):
    B, H, T, DK = q.shape
    nc = T // ML_CHUNK

    def to_chunks(a):
        return jnp.moveaxis(a.reshape(B, H, nc, ML_CHUNK, *a.shape[3:]), 2, 0)

    bcum = jnp.cumsum(lf.reshape(B, H, nc, ML_CHUNK), axis=-1)
    xs = (to_chunks(q), to_chunks(k), to_chunks(v), to_chunks(ig), jnp.moveaxis(bcum, 2, 0))
    mask = jnp.tril(jnp.ones((ML_CHUNK, ML_CHUNK), dtype=bool))

    def step(carry, inp):
        C, n, m = carry
        qc, kc, vc, igc, bc = inp
        d = bc[..., :, None] - bc[..., None, :] + igc[..., None, :]
        d = jnp.where(mask, d, -jnp.inf)
        g = bc + m[..., None]
        m_t = jnp.maximum(g, jnp.max(d, axis=-1))
        w_intra = jnp.exp(d - m_t[..., None])
        w_inter = jnp.exp(g - m_t)
        sw = jnp.einsum('bhtd,bhsd->bhts', qc, kc) * w_intra
        num = jnp.einsum('bhts,bhsv->bhtv', sw, vc) + w_inter[..., None] * jnp.einsum('bhtd,bhdv->bhtv', qc, C)
        den = jnp.sum(sw, axis=-1) + w_inter * jnp.einsum('bhtd,bhd->bht', qc, n)
        h = num / jnp.maximum(jnp.abs(den), jnp.exp(-m_t))[..., None]
        b_last = bc[..., -1]
        m_new = m_t[..., -1]
        a_prev = jnp.exp(b_last + m - m_new)
        w_s = jnp.exp(b_last[..., None] - bc + igc - m_new[..., None])
        kw = kc * w_s[..., None]
        C_new = a_prev[..., None, None] * C + jnp.einsum('bhsd,bhsv->bhdv', kw, vc)
        n_new = a_prev[..., None] * n + jnp.sum(kw, axis=2)
        return (C_new, n_new, m_new), h

    (C, n, m), h = lax.scan(step, (C0, n0, m0), xs)
    h = jnp.moveaxis(h, 0, 2).reshape(B, H, T, v.shape[-1])
    return h, C, n, m


def mla_expand(ckv_n, krope, w_ukv):
    B, T, _ = ckv_n.shape
    kv = (ckv_n @ w_ukv).reshape(B, T, MLA_HEADS, MLA_NOPE + MLA_V)
    k_rope = jnp.broadcast_to(krope[:, :, None, :].astype(kv.dtype), (B, T, MLA_HEADS, MLA_ROPE))
    return jnp.concatenate([kv[..., :MLA_NOPE], k_rope], axis=-1), kv[..., MLA_NOPE:]


def mixer_layer(x, mod, l, W, rope, ctx):
    B, T, _ = x.shape
    shift, scale, gate = jnp.split(mod, 3, axis=-1)
    h = rmsnorm(x, W['g_norm'][l]) * (1 + scale) + shift
    proj = h @ W['W_in'][l]
    (cq, ckv, krope, z_a, dq, dk, dv, z_b, mq, mk, mv, mo, z_c, mg) = jnp.split(proj, IN_SPLITS, axis=-1)

    q_a = (rmsnorm(cq, W['mla_q_norm'][l]) @ W['W_uq'][l]).reshape(B, T, MLA_HEADS, MLA_NOPE + MLA_ROPE)
    ckv_n = rmsnorm(ckv, W['mla_kv_norm'][l])
    if rope is not None:
        cos_a, sin_a = rope[0]
        q_a = jnp.concatenate([q_a[..., :MLA_NOPE], apply_rope(q_a[..., MLA_NOPE:], cos_a, sin_a)], axis=-1)
        krope_use = apply_rope(krope[:, :, None, :], cos_a, sin_a)[:, :, 0, :]
    else:
        krope_use = krope
    k_a, v_a = mla_expand(ckv_n, krope_use, W['W_ukv'][l])
    if ctx is not None:
        k_c, v_c = mla_expand(ctx['ckv'], ctx['krope'], W['W_ukv'][l])
        k_a = jnp.concatenate([k_a, k_c.astype(k_a.dtype)], axis=1)
        v_a = jnp.concatenate([v_a, v_c.astype(v_a.dtype)], axis=1)
    o_a = attention(q_a, k_a, v_a, MLA_SCALE)

    q_b = dq.reshape(B, T, DIFF_HEADS, 2 * DIFF_D)
    k_b = dk.reshape(B, T, DIFF_HEADS, 2 * DIFF_D)
    v_b = dv.reshape(B, T, DIFF_HEADS, 2 * DIFF_D)
    if rope is not None:
        cos_b, sin_b = rope[1]
        q_b = rope_sub(q_b, cos_b, sin_b)
        keys_b = rope_sub(k_b, cos_b, sin_b)
    else:
        keys_b = k_b
    vals_b = v_b
    if ctx is not None:
        keys_b = jnp.concatenate([keys_b, ctx['dk'].astype(keys_b.dtype)], axis=1)
        vals_b = jnp.concatenate([vals_b, ctx['dv'].astype(vals_b.dtype)], axis=1)
    lam_init = 0.8 - 0.6 * math.exp(-0.3 * l)
    lp = W['diff_lambda'][l].astype(jnp.float32)
    lam = jnp.exp(jnp.sum(lp[0] * lp[1])) - jnp.exp(jnp.sum(lp[2] * lp[3])) + lam_init
    o1 = attention(q_b[..., :DIFF_D], keys_b[..., :DIFF_D], vals_b, DIFF_SCALE)
    o2 = attention(q_b[..., DIFF_D:], keys_b[..., DIFF_D:], vals_b, DIFF_SCALE)
    o_b = rmsnorm(o1 - lam.astype(o1.dtype) * o2, W['diff_norm'][l]) * (1.0 - lam_init)

    qk = jax.nn.silu(short_conv(jnp.concatenate([mq, mk], axis=-1), W['ml_conv'][l]))
    mq_c, mk_c = qk[..., :GROUP_W], qk[..., GROUP_W:]
    f32 = jnp.float32
    qh = (mq_c * (ML_DK ** -0.5)).reshape(B, T, ML_HEADS, ML_DK).transpose(0, 2, 1, 3).astype(f32)
    kh = mk_c.reshape(B, T, ML_HEADS, ML_DK).transpose(0, 2, 1, 3).astype(f32)
    vh = mv.reshape(B, T, ML_HEADS, ML_DV).transpose(0, 2, 1, 3).astype(f32)
    gts = (mg + W['ml_gate_b'][l].reshape(N_ML_GATES).astype(mg.dtype)).astype(f32)
    gts = jnp.moveaxis(gts.reshape(B, T, 2, 2, ML_HEADS), 1, -1)
    ig_f, lf_f = gts[:, 0, 0], jax.nn.log_sigmoid(gts[:, 0, 1])
    ig_b, lf_b = gts[:, 1, 0], jax.nn.log_sigmoid(gts[:, 1, 1])
    if ctx is None:
        zC = jnp.zeros((B, ML_HEADS, ML_DK, ML_DV), f32)
        zn = jnp.zeros((B, ML_HEADS, ML_DK), f32)
        zm = jnp.zeros((B, ML_HEADS), f32)
        init_f, init_b = (zC, zn, zm), (zC, zn, zm)
    else:
        Cc, nc_, mc = ctx['C'].astype(f32), ctx['n'].astype(f32), ctx['m'].astype(f32)
        init_f = (Cc[:, 0], nc_[:, 0], mc[:, 0])
        init_b = (Cc[:, 1], nc_[:, 1], mc[:, 1])
    h_f, C_f, n_f, m_f = mlstm_scan(qh, kh, vh, ig_f, lf_f, *init_f)
    flip = lambda a: jnp.flip(a, axis=2)
    h_b, C_b, n_b, m_b = mlstm_scan(flip(qh), flip(kh), flip(vh), flip(ig_b), flip(lf_b), *init_b)
    h_c = (h_f + flip(h_b)).transpose(0, 2, 1, 3).astype(x.dtype)
    h_c = jax.nn.sigmoid(mo).reshape(B, T, ML_HEADS, ML_DV) * h_c
    o_c = rmsnorm(h_c, W['ml_norm'][l])

    y = jnp.concatenate([o_a.reshape(B, T, GROUP_W) * jax.nn.silu(z_a),
                         o_b.reshape(B, T, GROUP_W) * jax.nn.silu(z_b),
                         o_c.reshape(B, T, GROUP_W) * jax.nn.silu(z_c)], axis=-1) @ W['W_out'][l]
    x_new = x + gate * y
    side = (ckv_n, krope, k_b, v_b,
            jnp.stack([C_f, C_b], axis=1), jnp.stack([n_f, n_b], axis=1), jnp.stack([m_f, m_b], axis=1))
    return x_new, side


def setup_inputs(seed: int = 0) -> dict:
    key = jax.random.key(seed)
    ks = jax.random.split(key, 32)
    nrm = lambda k, shape, s: jax.random.normal(k, shape, jnp.float32) * s
    gain = lambda k, shape: 1.0 + 0.02 * jax.random.normal(k, shape, jnp.float32)
    gate_b = nrm(ks[21], (DEPTH, 2, 2, ML_HEADS), 0.1) + jnp.array([0.0, FORGET_BIAS], jnp.float32)[None, None, :, None]
    return {
        'x_prompt': nrm(ks[0], (BATCH, SEQ, D_MODEL), 1.0),
        'x_sample': nrm(ks[1], (DEC_BATCH, DEC_SEQ, D_MODEL), 1.0),
        'cache_mla_ckv': nrm(ks[2], (DEC_BATCH, DEPTH, PAST_LEN, MLA_KV_RANK), 1.0),
        'cache_mla_krope': nrm(ks[3], (DEC_BATCH, DEPTH, PAST_LEN, MLA_ROPE), 1.0),
        'cache_diff_k': nrm(ks[4], (DEC_BATCH, DEPTH, PAST_LEN, DIFF_HEADS, 2 * DIFF_D), 1.0),
        'cache_diff_v': nrm(ks[5], (DEC_BATCH, DEPTH, PAST_LEN, DIFF_HEADS, 2 * DIFF_D), 1.0),
        'state_mlstm_C': nrm(ks[6], (DEC_BATCH, DEPTH, 2, ML_HEADS, ML_DK, ML_DV), 1.0),
        'state_mlstm_n': nrm(ks[7], (DEC_BATCH, DEPTH, 2, ML_HEADS, ML_DK), 1.0),
        'state_mlstm_m': nrm(ks[8], (DEC_BATCH, DEPTH, 2, ML_HEADS), 1.0),
        'c': nrm(ks[9], (DEC_BATCH, D_MODEL), 1.0),
        'c_ctx': nrm(ks[10], (D_MODEL,), 1.0),
        'g_norm': gain(ks[11], (DEPTH, D_MODEL)),
        'W_mod': nrm(ks[12], (DEPTH, D_MODEL, 3 * D_MODEL), 0.5 * D_MODEL ** -0.5),
        'b_mod': nrm(ks[13], (DEPTH, 3 * D_MODEL), 0.02),
        'W_in': nrm(ks[14], (DEPTH, D_MODEL, N_IN), D_MODEL ** -0.5),
        'mla_q_norm': gain(ks[15], (DEPTH, MLA_Q_RANK)),
        'W_uq': nrm(ks[16], (DEPTH, MLA_Q_RANK, MLA_HEADS * (MLA_NOPE + MLA_ROPE)), MLA_Q_RANK ** -0.5),
        'mla_kv_norm': gain(ks[17], (DEPTH, MLA_KV_RANK)),
        'W_ukv': nrm(ks[18], (DEPTH, MLA_KV_RANK, MLA_HEADS * (MLA_NOPE + MLA_V)), MLA_KV_RANK ** -0.5),
        'diff_lambda': nrm(ks[19], (DEPTH, 4, DIFF_D), 0.1),
        'diff_norm': gain(ks[20], (DEPTH, 2 * DIFF_D)),
        'ml_conv': nrm(ks[22], (DEPTH, CONV_K, 2 * GROUP_W), CONV_K ** -0.5),
        'ml_gate_b': gate_b,
        'ml_norm': gain(ks[23], (DEPTH, ML_HEADS, ML_DV)),
        'W_out': nrm(ks[24], (DEPTH, D_MIX, D_MODEL), D_MIX ** -0.5),
        'g_final': gain(ks[25], (D_MODEL,)),
    }


def reference(x_prompt, x_sample, cache_mla_ckv, cache_mla_krope, cache_diff_k, cache_diff_v,
              state_mlstm_C, state_mlstm_n, state_mlstm_m, c, c_ctx, g_norm, W_mod, b_mod, W_in,
              mla_q_norm, W_uq, mla_kv_norm, W_ukv, diff_lambda, diff_norm, ml_conv, ml_gate_b,
              ml_norm, W_out, g_final):
    W = {'g_norm': g_norm, 'W_in': W_in, 'mla_q_norm': mla_q_norm, 'W_uq': W_uq,
         'mla_kv_norm': mla_kv_norm, 'W_ukv': W_ukv, 'diff_lambda': diff_lambda, 'diff_norm': diff_norm,
         'ml_conv': ml_conv, 'ml_gate_b': ml_gate_b, 'ml_norm': ml_norm, 'W_out': W_out}

    x = x_prompt
    sides = []
    for l in range(DEPTH):
        mod = (jax.nn.silu(c_ctx) @ W_mod[l] + b_mod[l])[None, None, :]
        x, side = mixer_layer(x, mod, l, W, None, None)
        sides.append(side)
    y_prompt = rmsnorm(x, g_final)
    new_mla_ckv = jnp.stack([s[0] for s in sides], axis=1)
    new_mla_krope = jnp.stack([s[1] for s in sides], axis=1)
    new_diff_k = jnp.stack([s[2] for s in sides], axis=1)
    new_diff_v = jnp.stack([s[3] for s in sides], axis=1)
    new_mlstm_C = jnp.stack([s[4] for s in sides], axis=1)
    new_mlstm_n = jnp.stack([s[5] for s in sides], axis=1)
    new_mlstm_m = jnp.stack([s[6] for s in sides], axis=1)

    n_rows = x_sample.shape[1] // GRID_W
    rope = (axial_rope_tables(n_rows, MLA_ROPE), axial_rope_tables(n_rows, DIFF_D))
    x = x_sample
    for l in range(DEPTH):
        mod = (jax.nn.silu(c) @ W_mod[l] + b_mod[l])[:, None, :]
        ctx = {'ckv': cache_mla_ckv[:, l], 'krope': cache_mla_krope[:, l],
               'dk': cache_diff_k[:, l], 'dv': cache_diff_v[:, l],
               'C': state_mlstm_C[:, l], 'n': state_mlstm_n[:, l], 'm': state_mlstm_m[:, l]}
        x, _ = mixer_layer(x, mod, l, W, rope, ctx)
    y_sample = rmsnorm(x, g_final)

    return (y_prompt, y_sample, new_mla_ckv, new_mla_krope, new_diff_k, new_diff_v,
            new_mlstm_C, new_mlstm_n, new_mlstm_m)
```

```python
import math
import contextlib
import numpy as np
import concourse.bass as bass
import concourse.mybir as mybir
from concourse.bass_utils import run_bass_kernel_spmd

F32 = mybir.dt.float32
BF16 = mybir.dt.bfloat16
AF = mybir.ActivationFunctionType
ALU = mybir.AluOpType
AX = mybir.AxisListType

ENGS = ("pe", "act", "dve", "pool", "sp")
STRICT = True


def _region(ap):
    t = ap.tensor
    es = mybir.dt.size(ap.dtype)
    pat = list(ap.ap)
    off = int(ap.offset)
    if "DRAM" in str(ap.space).upper():
        lo = off
        hi = off + sum((c - 1) * abs(s) for s, c in pat) + 1
        return (t.name, 0, 1, lo * es, hi * es)
    pcnt = pat[0][1]
    per_part = 1
    for d in list(t.shape)[1:]:
        per_part *= int(d)
    p0 = off // per_part
    f0 = off % per_part
    ext = sum((c - 1) * abs(s) for s, c in pat[1:]) + 1
    if "PSUM" in str(ap.space).upper():
        b0 = (f0 * es) // 2048 * 2048
        b1 = -(-((f0 + ext) * es) // 2048) * 2048
        return (t.name, p0 // 32 * 32, -(-(p0 + pcnt) // 32) * 32, b0, b1, True)
    return (t.name, p0, p0 + pcnt, f0 * es, (f0 + ext) * es)


def _overlap(a, b):
    return a[0] == b[0] and a[1] < b[2] and b[1] < a[2] and a[3] < b[4] and b[3] < a[4]


def _covers(a, b):
    return a[0] == b[0] and a[1] <= b[1] and a[2] >= b[2] and a[3] <= b[3] and a[4] >= b[4]


class Op:
    __slots__ = ("eng", "fn", "reads", "writes", "deps", "idx", "sig", "dma")

    def __init__(self, eng, fn, reads, writes, dma=False):
        self.eng, self.fn, self.dma = eng, fn, dma
        self.reads = [_region(a) for a in reads if a is not None]
        self.writes = [_region(a) for a in writes if a is not None]
        self.deps = set()
        self.sig = 0


class Prog:
    NROT = 24

    def __init__(self, nc):
        self.nc = nc
        self.ops = []
        self.acc = {}
        self.phase = "init"
        self.phases = []

    def op(self, eng, fn, reads=(), writes=(), dma=False):
        o = Op(eng, fn, reads, writes, dma)
        o.idx = len(self.ops)
        self.phases.append(self.phase)
        self.ops.append(o)
        for r in o.reads:
            ps = len(r) > 5
            for (reg, oi, w, en) in self.acc.setdefault(r[0], []):
                if (w or (ps and en != eng)) and _overlap(reg, r):
                    o.deps.add(oi)
        for r in o.writes:
            for (reg, oi, w, en) in self.acc.setdefault(r[0], []):
                if _overlap(reg, r):
                    o.deps.add(oi)
        for r in o.writes:
            lst = self.acc[r[0]]
            lst[:] = [e for e in lst if not _covers(r, e[0])]
            lst.append((r, o.idx, True, eng))
        for r in o.reads:
            self.acc[r[0]].append((r, o.idx, False, eng))
        o.deps.discard(o.idx)
        return o

    def emit(self):
        nc, ops, NROT = self.nc, self.ops, self.NROT
        need = [None] * len(ops)
        signal = [False] * len(ops)
        for o in ops:
            best = {}
            for d in o.deps:
                p = ops[d]
                if p.dma:
                    best[("dma", d)] = d
                    continue
                if p.eng == o.eng:
                    if o.eng in ("pe", "sp") or o.dma:
                        continue
                    if not STRICT and not any(_overlap(w, r) for w in p.writes for r in o.reads):
                        continue
                if best.get(p.eng, -1) < d:
                    best[p.eng] = d
            need[o.idx] = list(best.values())
            for d in need[o.idx]:
                signal[d] = True
        cnt = {e: 0 for e in ENGS}
        dcnt = {e: 0 for e in ENGS}
        for o in ops:
            if o.dma:
                o.sig = dcnt[o.eng]
                dcnt[o.eng] += 1
            elif signal[o.idx]:
                cnt[o.eng] += 1
                o.sig = cnt[o.eng]
        per_eng = {e: [] for e in ENGS}
        for o in ops:
            per_eng[o.eng].append(o)
        with contextlib.ExitStack() as st:
            sems = {e: st.enter_context(nc.semaphore("s_" + e)) for e in ENGS}
            dsems = {}
            for e in ENGS:
                if dcnt[e] > 0:
                    dsems[e] = [st.enter_context(nc.semaphore("d_%s_%d" % (e, i)))
                                for i in range(min(NROT, dcnt[e]))]
            block = st.enter_context(nc.Block())
            engobj = {"pe": "tensor", "act": "scalar", "dve": "vector", "pool": "gpsimd", "sp": "sync"}

            def run(ename, e):
                known = {}

                def wait(key, s, v):
                    if known.get(key, 0) >= v:
                        return
                    known[key] = v
                    e.wait_ge(s, v)

                for o in per_eng[ename]:
                    for d in need[o.idx]:
                        p = ops[d]
                        if p.dma:
                            slot = p.sig % NROT
                            wait((p.eng, "d", slot), dsems[p.eng][slot], 16 * (p.sig // NROT + 1))
                        else:
                            wait((p.eng, "c"), sems[p.eng], p.sig)
                    if o.dma and o.sig >= NROT:
                        slot = o.sig % NROT
                        wait((ename, "d", slot), dsems[ename][slot], 16 * (o.sig // NROT))
                    ins = o.fn(e)
                    if o.dma:
                        ins.then_inc(dsems[ename][o.sig % NROT], 16)
                    elif signal[o.idx]:
                        ins.then_inc(sems[ename], 1)
                n = dcnt[ename]
                for slot in range(min(NROT, n)):
                    last = ((n - 1 - slot) // NROT) * NROT + slot
                    wait((ename, "d", slot), dsems[ename][slot], 16 * (last // NROT + 1))

            for ename in ENGS:
                if not per_eng[ename]:
                    continue

                def body(e, ename=ename):
                    run(ename, e)
                getattr(block, engobj[ename])(body)


T = 1024
NT = 8
L = 4
NKT = 10
TK = 1280
EPS = 1e-6
MLA_SCALE = 192 ** -0.5
DIFF_SCALE = 64 ** -0.5
NEG = -30000.0

W_SPECS = [("g_norm", [L, 1024]), ("W_mod", [L, 1024, 3072]), ("b_mod", [L, 3072]),
           ("W_in", [L, 1024, 5840]), ("mla_q_norm", [L, 384]), ("W_uq", [L, 384, 768]),
           ("mla_kv_norm", [L, 256]), ("W_ukv", [L, 256, 1024]), ("diff_lambda", [L, 256]),
           ("diff_norm", [L, 128]), ("ml_conv", [L, 3, 1024]), ("ml_gate_b", [L, 16]),
           ("ml_norm", [L, 512]), ("W_out", [L, 1536, 1024]), ("g_final", [1024])]
IN_SPECS = [("x", [T, 1024]), ("cvec", [1024]), ("cckv", [L, 256, 256]), ("ckrope", [L, 256, 64]),
            ("cdk", [L, 256, 512]), ("cdv", [L, 256, 512]), ("sC", [L, 2, 4, 128, 128]),
            ("sn", [L, 2, 4, 128]), ("sm", [L, 8]), ("maskb", [128, 40]), ("keep", [128, 4]),
            ("kbar", [128, 1]), ("cos2", [128, T]), ("sin2", [128, T]), ("consts", [128, 6, 128])]
OUT_SPECS = [("y", [T, 1024]), ("o_ckv", [L, T, 256]), ("o_krope", [L, T, 64]), ("o_dk", [L, T, 512]),
             ("o_dv", [L, T, 512]), ("o_C", [L, 4, 2, 4, 128, 128]), ("o_n", [L, 4, 2, 4, 128]),
             ("o_m", [L, 4, 2, 4])]


class _Stop(Exception):
    pass


def build(nlayers=L, stop=None):
    nc = bass.Bass("TRN2", target_bir_lowering=False)
    P = Prog(nc)
    D = {}
    for n, s in W_SPECS + IN_SPECS:
        D[n] = nc.dram_tensor(n, s, F32, kind="ExternalInput").ap()
    for n, s in OUT_SPECS:
        D[n] = nc.dram_tensor(n, s, F32, kind="ExternalOutput").ap()

    def sb(name, shape, dt=F32):
        return nc.alloc_sbuf_tensor("s_" + name, shape, dt)

    def isap(v):
        return v is not None and not isinstance(v, (int, float))

    def mm(out, lhsT, rhs, start=True, stop=True, skip=False):
        P.op("pe", lambda e: e.matmul(out, lhsT=lhsT, rhs=rhs, start=start, stop=stop,
                                      skip_group_check=skip),
             [lhsT, rhs], [out])

    def tr(out, in_, ident):
        P.op("pe", lambda e: e.transpose(out, in_, ident), [in_, ident], [out])

    def act(out, in_, func, bias=None, scale=None, accum=None):
        kw = {}
        reads = [in_]
        if bias is not None:
            kw["bias"] = bias
            if isap(bias):
                reads.append(bias)
        if scale is not None:
            kw["scale"] = scale
            if isap(scale):
                reads.append(scale)
        if accum is not None:
            kw["accum_out"] = accum
        P.op("act", lambda e: e.activation(out=out, in_=in_, func=func, **kw), reads,
             [out] + ([accum] if accum is not None else []))

    def tt(out, a, b, op, eng="dve"):
        P.op(eng, lambda e: e.tensor_tensor(out=out, in0=a, in1=b, op=op), [a, b], [out])

    def ts(out, a, s1, op0, s2=None, op1=None, eng="dve"):
        reads = [a] + [s for s in (s1, s2) if isap(s)]
        if op1 is None:
            P.op(eng, lambda e: e.tensor_scalar(out=out, in0=a, scalar1=s1, scalar2=None, op0=op0), reads, [out])
        else:
            P.op(eng, lambda e: e.tensor_scalar(out=out, in0=a, scalar1=s1, scalar2=s2, op0=op0, op1=op1),
                 reads, [out])

    def stt(out, a, s, b, op0, op1):
        reads = [a, b] + ([s] if isap(s) else [])
        P.op("dve", lambda e: e.scalar_tensor_tensor(out=out, in0=a, scalar=s, in1=b, op0=op0, op1=op1),
             reads, [out])

    def cp(out, in_, eng="dve"):
        if eng == "act":
            P.op("act", lambda e: e.copy(out=out, in_=in_), [in_], [out])
        else:
            P.op(eng, lambda e: e.tensor_copy(out=out, in_=in_), [in_], [out])

    def red(out, in_, op):
        P.op("dve", lambda e: e.tensor_reduce(out=out, in_=in_, axis=AX.X, op=op), [in_], [out])

    def recip(out, in_):
        P.op("dve", lambda e: e.reciprocal(out=out, in_=in_), [in_], [out])

    def memset(ap, v, eng="pool"):
        P.op(eng, lambda e: e.memset(ap, v), [], [ap])

    def dma(out, in_, eng="sp"):
        P.op(eng, lambda e: e.dma_start(out=out, in_=in_, allow_slow_non_contiguous=True), [in_], [out], dma=True)

    def rsqrt_inplace(ap, scale, n=None):
        act(ap, ap, AF.Ln, bias=eps_c[0:ap.shape[0], :], scale=scale)
        act(ap, ap, AF.Exp, scale=-0.5)

    psA = nc.alloc_psum_tensor("psA", [128, 1024], F32)
    psB = nc.alloc_psum_tensor("psB", [128, 1024], F32)
    psO = nc.alloc_psum_tensor("psO", [128, 1024], F32)
    psD = nc.alloc_psum_tensor("psD", [128, 1024], F32)
    banks = [psA[:, 0:512], psA[:, 512:1024], psB[:, 0:512], psB[:, 512:1024], psO[:, 0:512], psO[:, 512:1024]]
    xbanks = [psD[:, 0:512], psD[:, 512:1024]]
    bctr = [0, 0]

    def gbank():
        b = banks[bctr[0] % 6]
        bctr[0] += 1
        return b

    def xbank():
        b = xbanks[bctr[1] % 2]
        bctr[1] += 1
        return b

    cst_f = sb("cst_f", [128, 6, 128])
    cst_b = sb("cst_b", [128, 6, 128], BF16)
    dma(cst_f[:, :, :], D["consts"][:, :, :])
    cp(cst_b[:, :, :], cst_f[:, :, :])
    ident_f, ident_b = cst_f[:, 0, :], cst_b[:, 0, :]
    pswap_b = cst_b[:, 1, :]
    maskF_f, maskB_f = cst_f[:, 2, :], cst_f[:, 3, :]
    selA_f, selB_f = cst_f[:, 4, :], cst_f[:, 5, :]
    ones_b = sb("ones_b", [128, 128], BF16)
    memset(ones_b[:, :], 1.0)
    ones_f = sb("ones_f", [128, 128])
    memset(ones_f[:, :], 1.0)
    eps_c = sb("eps_c", [128, 1])
    memset(eps_c[:, :], EPS)
    one_c = sb("one_c", [128, 1])
    memset(one_c[:, :], 1.0)
    lnq_c = sb("lnq_c", [128, 1])
    memset(lnq_c[:, :], math.log(128 ** -0.5))
    nlnq_c = sb("nlnq_c", [128, 1])
    memset(nlnq_c[:, :], -math.log(128 ** -0.5))
    mask4 = sb("mask4", [128, 2, 4, 128], BF16)
    for h in range(4):
        cp(mask4[:, 0, h, :], maskF_f)
        cp(mask4[:, 1, h, :], maskB_f)
    cos2 = sb("cos2", [128, T], BF16)
    sin2 = sb("sin2", [128, T], BF16)
    dma(cos2[:, :], D["cos2"][:, :], eng="pool")
    dma(sin2[:, :], D["sin2"][:, :], eng="pool")
    maskb = sb("maskb", [128, 40])
    dma(maskb[:, :], D["maskb"][:, :])
    keep = sb("keep", [128, 4])
    dma(keep[:, :], D["keep"][:, :])
    kbar = sb("kbar", [128, 1])
    dma(kbar[:, :], D["kbar"][:, :])

    gnorm = sb("gnorm", [128, L, 8])
    bmod = sb("bmod", [128, L, 24])
    gq = sb("gq", [128, L, 3])
    gkv = sb("gkv", [128, 1, 256])
    dlam = sb("dlam", [128, 1, 256])
    dnorm = sb("dnorm", [128, L])
    wconv = sb("wconv", [128, L, 3, 8])
    gateb = sb("gateb", [128, L, 16])
    mlnorm = sb("mlnorm", [128, 1, 512])
    gfin = sb("gfin", [128, 8])
    for l in range(L):
        dma(gnorm[:, l, :], D["g_norm"][l].rearrange("(kt p) -> p kt", p=128))
        dma(bmod[:, l, :], D["b_mod"][l].rearrange("(kt p) -> p kt", p=128))
        dma(gq[:, l, :], D["mla_q_norm"][l].rearrange("(kt p) -> p kt", p=128))
        dma(dnorm[:, l:l + 1], D["diff_norm"][l].rearrange("(p o) -> p o", o=1))
        for k in range(3):
            dma(wconv[:, l, k, :], D["ml_conv"][l, k].rearrange("(ct p) -> p ct", p=128))
    dma(gateb[:, :, :], D["ml_gate_b"].rearrange("(o l) c -> o l c", o=1).partition_broadcast(128))
    dma(gfin[:, :], D["g_final"].rearrange("(kt p) -> p kt", p=128))

    xT = sb("xT", [128, 8, T])
    hT = sb("hT", [128, 8, T], BF16)
    zT = sb("zT", [128, 4, T], BF16)
    yT = zT
    wbufs = [sb("wbuf%d" % i, [128, 8, 512], BF16) for i in range(3)]
    wctr = [0]
    attQ = sb("attQ", [128, 4, T], BF16)
    attK = sb("attK", [128, 4, TK], BF16)
    attV = sb("attV", [128, NKT, 512], BF16)
    sh8 = sb("sh8", [128, 4 * T], BF16)
    qropeT = sh8[0:64, :].rearrange("p (a b) -> p a b", a=4)
    qrope128 = sh8[:, :].rearrange("p (a b) -> p a b", a=4)
    kropeT_full = sb("kropeT", [128, TK], BF16)
    kropeT = kropeT_full[0:64, :]
    memset(kropeT_full[64:128, :], 0.0)
    ckvnT = sb("ckvnT", [128, 2, TK], BF16)
    rstd = sb("rstd", [128, T])
    tmpA = sb("tmpA", [128, T])
    tmpB = sb("tmpB", [128, T])
    osb = sb("osb", [128, T])
    ddbuf = [tmpB[:, 0:512], sb("ddb1", [128, 512])]
    ptb = [sb("pt%d" % i, [128, 1024], BF16) for i in range(2)]
    sqb = [sb("sq%d" % i, [128, 512], BF16) for i in range(2)]
    sqc = [0]

    def sqbuf():
        b = sqb[sqc[0] % 2]
        sqc[0] += 1
        return b

    xin = [tmpA, tmpB]
    for t_ in range(NT):
        xi = xin[t_ % 2]
        dma(xi[:, :], D["x"][t_ * 128:(t_ + 1) * 128, :])
        for half in range(2):
            pb = gbank()
            for j in range(4):
                kt = half * 4 + j
                tr(pb[:, j * 128:(j + 1) * 128], xi[:, kt * 128:(kt + 1) * 128], ident_f)
            for j in range(4):
                kt = half * 4 + j
                cp(xT[:, kt, t_ * 128:(t_ + 1) * 128], pb[:, j * 128:(j + 1) * 128],
                   eng="act" if j % 2 else "dve")

    def load_w(src2d, nk, ncols):
        wb = wbufs[wctr[0] % 3]
        wctr[0] += 1
        flat = wb[:, :, :].rearrange("p a b -> p (a b)")[:, 0:nk * ncols]
        view = flat.rearrange("p (a b) -> p a b", a=nk)
        dma(view, src2d.rearrange("(kt p) c -> p kt c", p=128), eng="pool")
        return view

    cv = sb("cv", [128, 8])
    dma(cv[:, :], D["cvec"].rearrange("(kt p) -> p kt", p=128))
    cs_b = sb("cs_b", [128, 8], BF16)
    act(cs_b[:, :], cv[:, :], AF.Silu)
    modT = sb("modT", [128, L, 24])
    s1 = sb("s1", [128, L, 8])
    def emit_mod(l):
        pb = xbank()
        for blk in range(6):
            w = load_w(D["W_mod"][l][:, blk * 512:(blk + 1) * 512], 8, 512)
            for mi in range(4):
                col = blk * 4 + mi
                for kt in range(8):
                    mm(pb[:, col:col + 1], w[:, kt, mi * 128:(mi + 1) * 128], cs_b[:, kt:kt + 1],
                       start=(kt == 0), stop=(kt == 7))
        tt(modT[:, l, :], pb[:, 0:24], bmod[:, l, :], ALU.add)
        stt(s1[:, l, :], modT[:, l, 8:16], 1.0, gnorm[:, l, :], ALU.add, ALU.mult)

    def mod_block_load(l, blk):
        return load_w(D["W_mod"][l][:, blk * 512:(blk + 1) * 512], 8, 512)

    def mod_block_compute(l, blk, w):
        pb = psB[:, 512:1024]
        for mi in range(4):
            for kt in range(8):
                mm(pb[:, 16 + mi:17 + mi], w[:, kt, mi * 128:(mi + 1) * 128], cs_b[:, kt:kt + 1],
                   start=(kt == 0), stop=(kt == 7))
        tt(modT[:, l, blk * 4:(blk + 1) * 4], pb[:, 16:20], bmod[:, l, blk * 4:(blk + 1) * 4], ALU.add)
        if blk == 5:
            stt(s1[:, l, :], modT[:, l, 8:16], 1.0, gnorm[:, l, :], ALU.add, ALU.mult)

    if nlayers > 0:
        emit_mod(0)

    def featmaj_proj(l, lo, n, consumer, w=None):
        if w is None:
            w = load_w(D["W_in"][l][:, lo:lo + n], 8, n)
        nm = (n + 127) // 128
        pending = None
        for ch in range(2):
            for mi in range(nm):
                m = min(128, n - mi * 128)
                pb = gbank()
                for kt in range(8):
                    mm(pb[0:m, :], w[:, kt, mi * 128:mi * 128 + m], hT[:, kt, ch * 512:(ch + 1) * 512],
                       start=(kt == 0), stop=(kt == 7))
                if pending is not None:
                    pending()
                pending = consumer(mi, ch, pb[0:m, :])
        if pending is not None:
            pending()

    def tokmaj_proj(l, lo, n, consumer, w=None):
        if w is None:
            w = load_w(D["W_in"][l][:, lo:lo + n], 8, n)
        pending = None
        for t_ in range(NT):
            pb = gbank()
            for kt in range(8):
                mm(pb[:, 0:n], hT[:, kt, t_ * 128:(t_ + 1) * 128], w[:, kt, :],
                   start=(kt == 0), stop=(kt == 7))
            if pending is not None:
                pending()
            pending = consumer(t_, pb[:, 0:n])
        if pending is not None:
            pending()

    ropet = [sb("ropet%d" % i, [128, 512]) for i in range(2)]

    ropeb = [ropet[i][:, :].bitcast(BF16) for i in range(2)]
    ropectr = [0]

    def rope(out, raw, npart, tok0, ntok, out_hi=None):
        pb = gbank()
        mm(pb[0:npart, 0:ntok], pswap_b[0:npart, 0:npart], raw, start=True, stop=True)
        rb_ = ropeb[ropectr[0] % 2]
        ropectr[0] += 1
        t1, t2 = rb_[:, 0:512], rb_[:, 512:1024]
        tt(t1[0:npart, 0:ntok], raw, cos2[0:npart, tok0:tok0 + ntok], ALU.mult)
        tt(t2[0:npart, 0:ntok], pb[0:npart, 0:ntok], sin2[0:npart, tok0:tok0 + ntok], ALU.mult)
        if out_hi is None:
            tt(out, t1[0:npart, 0:ntok], t2[0:npart, 0:ntok], ALU.add)
        else:
            tt(out[0:64, :], t1[0:64, 0:ntok], t2[0:64, 0:ntok], ALU.add)
            tt(out_hi, t1[64:128, 0:ntok], t2[64:128, 0:ntok], ALU.add)

    def out_part(l, g):
        wv_ = []
        for half in range(2):
            wv_.append(load_w(D["W_out"][l][g * 512:(g + 1) * 512, half * 512:(half + 1) * 512], 4, 512))
        for ch in range(2):
            for mo in range(8):
                w = wv_[mo // 4]
                pb = gbank()
                for kt in range(4):
                    mm(pb[:, :], w[:, kt, (mo % 4) * 128:(mo % 4 + 1) * 128], yT[:, kt, ch * 512:(ch + 1) * 512],
                       start=(kt == 0), stop=(kt == 3))
                xs = xT[:, mo, ch * 512:(ch + 1) * 512]
                stt(xs, pb[:, :], modT[:, l, 16 + mo:17 + mo], xs, ALU.mult, ALU.add)

    def attention(scores_fn, v_fn, scale, post_fn):
        iters = [(qs, kt) for qs in range(4) for kt in range(NKT)]

        def emit_scores(i):
            qs, kt = iters[i]
            pS = psA if i % 2 == 0 else psB
            for u in range(4):
                scores_fn(u, kt, qs, pS[:, u * 256:(u + 1) * 256], first=(u % 2 == 0))

        def emit_exp(i):
            qs, kt = iters[i]
            pS = psA if i % 2 == 0 else psB
            act(ptb[i % 2][:, :], pS[:, :], AF.Exp, bias=maskb[:, kt * 4 + qs:kt * 4 + qs + 1], scale=scale)

        deferred = []
        emit_scores(0)
        emit_exp(0)
        for i, (qs, kt) in enumerate(iters):
            PT = ptb[i % 2]
            if i + 1 < len(iters):
                emit_scores(i + 1)
            for u in range(4):
                mm(psO[:, u * 256:(u + 1) * 256], v_fn(u, kt), PT[:, u * 256:(u + 1) * 256],
                   start=(kt == 0 and u % 2 == 0), stop=(kt == NKT - 1), skip=True)
            for half in range(2):
                mm(psD[:, half * 512:(half + 1) * 512], ones_b[:, :], PT[:, half * 512:(half + 1) * 512],
                   start=(kt == 0), stop=(kt == NKT - 1))
            if i + 1 < len(iters):
                emit_exp(i + 1)
            while deferred and deferred[0][0] <= i:
                deferred.pop(0)[1]((psA if i % 2 == 0 else psB)[:, 0:512])
            if kt == NKT - 1:
                act(tmpA[:, :], psD[:, :], AF.Ln)
                cp(osb[:, :], psO[:, :])

                def stage1(_pb, qs=qs, i=i):
                    act(tmpA[:, :], tmpA[:, :], AF.Exp, scale=-1.0)
                    tt(osb[:, :], osb[:, :], tmpA[:, :], ALU.mult)
                    st2 = post_fn(qs, osb)
                    if st2 is not None:
                        deferred.append((i + 4, st2))
                deferred.append((i + 2, stage1))
        while deferred:
            deferred.pop(0)[1](xbank())

    gx = sb("gx", [128, NT, 16])
    spf = sb("spf", [128, NT, 8])
    cs = sb("cs", [128, NT, 24])
    gsc = sb("gsc", [128, NT, 24])
    alast = sb("alast", [128, NT, 16])
    uu = sb("uu", [128, NT, 8])
    ktok = sh8[:, :].rearrange("p (t h d) -> p t h d", t=NT, h=4)
    vaug = attV[:, :, :].rearrange("p a b -> p (a b)")[:, 0:NT * 4 * 132].rearrange("p (t h d) -> p t h d", t=NT, h=4)
    vB = [sb("vB%d" % i, [128, 4, 132], BF16) for i in range(2)]
    Cst = sb("Cst", [128, 8, 130])
    Cbs = [sb("Cb%d" % i, [128, 8, 130], BF16) for i in range(2)]
    hsum = sb("hsum", [128, NT, 512], BF16)
    uraw = [sb("uraw%d" % i, [128, T + 2], BF16) for i in range(2)]
    cacc = tmpA
    for i in range(2):
        memset(uraw[i][:, 0:1], 0.0)
        memset(uraw[i][:, T + 1:T + 2], 0.0)
    ptm = [sb("ptm%d" % i, [128, 512], BF16) for i in range(2)]
    dtmp = sb("dtmp", [128, 2, 16])
    em0 = sb("em0", [128, 8])
    mch = sb("mch", [4, 2, 20])
    umax = sb("umax", [4, 2, 16])
    blast = sb("blast", [4, 2, 16])
    mout = sb("mout", [4, 8])
    mdiag = sb("mdiag", [4, 8, 4])
    emout = sb("emout", [128, 32])
    cout = sb("cout", [128, 1, 4, 130])
    tok_tmp = [sb("toktmp%d" % i, [128, 512]) for i in range(2)]
    tkc = [0]

    def toktmp():
        b = tok_tmp[tkc[0] % 2]
        tkc[0] += 1
        return b

    negw = sb("negw", [128, 2, 8])
    nlam = sb("nlam", [128, 4])
    gdl = sb("gdl", [128, 1])
    ssk = sb("ssk", [128, 8])
    ssm = sb("ssm", [128, 4])
    ctxf = sb("ctxf", [128, 2, 512])
    krraw = ddbuf[1].bitcast(BF16)[0:64, :]

    try:
      for l in range(nlayers):
          dma(gkv[:, 0, :], D["mla_kv_norm"][l:l + 1, :].partition_broadcast(128))
          dma(dlam[:, 0, :], D["diff_lambda"][l:l + 1, :].partition_broadcast(128))
          dma(mlnorm[:, 0, :], D["ml_norm"][l:l + 1, :].partition_broadcast(128))
          P.phase = "s1"
          for ch in range(2):
              pb = xbank()
              for kt in range(8):
                  sq = sqbuf()
                  act(sq[:, :], xT[:, kt, ch * 512:(ch + 1) * 512], AF.Square)
                  mm(pb[:, :], ones_b[:, :], sq[:, :], start=(kt == 0), stop=(kt == 7))
              act(rstd[:, ch * 512:(ch + 1) * 512], pb[:, :], AF.Ln, bias=eps_c[:, :], scale=1.0 / 1024)
              act(rstd[:, ch * 512:(ch + 1) * 512], rstd[:, ch * 512:(ch + 1) * 512], AF.Exp, scale=-0.5)
          for kt in range(8):
              stt(tmpA[:, :], xT[:, kt, :], s1[:, l, kt:kt + 1], rstd[:, :], ALU.mult, ALU.mult)
              act(hT[:, kt, :], tmpA[:, :], AF.Identity, bias=modT[:, l, kt:kt + 1], scale=1.0)

          if stop == "s1":
              raise _Stop()
          P.phase = "conv"
          dma(em0[:, :], D["sm"][l:l + 1, :].partition_broadcast(128))
          for d_ in range(2):
              dma(mch[:, d_, 0:1], D["sm"][l, d_ * 4:(d_ + 1) * 4].rearrange("(p o) -> p o", o=1))
          for d_ in range(2):
              for h in range(4):
                  dma(Cst[:, d_ * 4 + h, 0:128], D["sC"][l, d_, h])
                  dma(Cst[:, d_ * 4 + h, 128:129], D["sn"][l, d_, h].rearrange("(p o) -> p o", o=1))
          ts(negw[:, 0, :], wconv[:, l, 0, :], kbar[:, 0:1], ALU.mult, -1.0, ALU.mult)
          ts(negw[:, 1, :], wconv[:, l, 2, :], kbar[:, 0:1], ALU.mult, -1.0, ALU.mult)
          def mg_consumer(t_, ps):
              tt(gx[:, t_, :], ps, gateb[:, l, :], ALU.add)
          tokmaj_proj(l, 5824, 16, mg_consumer)
          gx5 = gx[:, :, :].rearrange("p t (d i h) -> p t d i h", d=2, i=2)
          spv = spf[:, :, :].rearrange("p t (d h) -> p t d h", d=2)
          for d_ in range(2):
              act(spv[:, :, d_, :], gx5[:, :, d_, 1, :], AF.Exp, scale=-1.0)
          act(spf[:, :, :], spf[:, :, :], AF.Ln, bias=one_c[:, :], scale=1.0)
          def gates_part_b():
              pb = xbank()
              for t_ in range(NT):
                  c0 = t_ * 24
                  mm(pb[:, c0:c0 + 4], maskF_f, spf[:, t_, 0:4], start=True, stop=True)
                  mm(pb[:, c0 + 4:c0 + 8], maskB_f, spf[:, t_, 4:8], start=True, stop=True)
                  mm(pb[:, c0 + 8:c0 + 16], selA_f, spf[:, t_, :], start=True, stop=True)
                  mm(pb[:, c0 + 16:c0 + 24], selB_f, spf[:, t_, :], start=True, stop=True)
              cp(cs[:, :, :], pb[:, 0:NT * 24].rearrange("p (t c) -> p t c", c=24))
              uuv = uu[:, :, :].rearrange("p t (d h) -> p t d h", d=2)
              csv = cs[:, :, 0:8].rearrange("p t (d h) -> p t d h", d=2)
              for d_ in range(2):
                  tt(uuv[:, :, d_, :], gx5[:, :, d_, 0, :], csv[:, :, d_, :], ALU.add)
              act(gsc[:, :, 0:8], uu[:, :, :], AF.Exp)
              act(gsc[:, :, 8:16], cs[:, :, 0:8], AF.Exp, bias=lnq_c[:, :], scale=-1.0)
              act(gsc[:, :, 16:24], cs[:, :, 0:8], AF.Exp, bias=nlnq_c[:, :], scale=1.0)
              act(alast[:, :, :], cs[:, :, 8:24], AF.Exp, scale=-1.0)


          for (lo, dst, ncap) in ((3264, attQ, T), (3776, attK, TK)):
              w = load_w(D["W_in"][l][:, lo:lo + 512], 8, 512)
              for mi in range(4):
                  ur = uraw[mi % 2]
                  cacc = tmpA if mi % 2 == 0 else tmpB
                  for ch in range(2):
                      pb = gbank()
                      for kt in range(8):
                          mm(pb[:, :], w[:, kt, mi * 128:(mi + 1) * 128], hT[:, kt, ch * 512:(ch + 1) * 512],
                             start=(kt == 0), stop=(kt == 7))
                      cp(ur[:, 1 + ch * 512:1 + (ch + 1) * 512], pb[:, :], eng="act")
                  ct = (lo - 3264) // 128 + mi
                  ts(cacc[:, :], ur[:, 1:T + 1], wconv[:, l, 1, ct:ct + 1], ALU.mult)
                  stt(cacc[:, :], ur[:, 0:T], wconv[:, l, 0, ct:ct + 1], cacc[:, :], ALU.mult, ALU.add)
                  stt(cacc[:, :], ur[:, 2:T + 2], wconv[:, l, 2, ct:ct + 1], cacc[:, :], ALU.mult, ALU.add)
                  cview = cacc[:, :].rearrange("p (a b) -> p a b", b=256)
                  uview = ur[:, 0:T].rearrange("p (a b) -> p a b", b=256)
                  uview2 = ur[:, 2:T + 2].rearrange("p (a b) -> p a b", b=256)
                  stt(cview[:, 1:4, 0], uview[:, 1:4, 0], negw[:, 0, ct:ct + 1], cview[:, 1:4, 0], ALU.mult, ALU.add)
                  stt(cview[:, 0:3, 255], uview2[:, 0:3, 255], negw[:, 1, ct:ct + 1], cview[:, 0:3, 255],
                      ALU.mult, ALU.add)
                  act(dst[:, mi, 0:T], cacc[:, :], AF.Silu)
              if lo == 3264:
                  gates_part_b()
          if stop == "conv":
              raise _Stop()
          qT, kT = attQ, attK

          P.phase = "mvgates"
          memset(vaug[0:64, :, :, 128:129], 1.0)
          memset(vaug[64:128, :, :, 128:129], 0.0)
          memset(vaug[0:64, :, :, 129:130], 0.0)
          memset(vaug[64:128, :, :, 129:130], 1.0)
          memset(vaug[:, :, :, 130:132], 1.0)
          def mv_consumer(t_, ps):
              cp(vaug[:, t_, :, 0:128], ps.rearrange("p (h d) -> p h d", h=4), eng="act")
          tokmaj_proj(l, 4288, 512, mv_consumer)

          if stop == "gates":
              raise _Stop()
          for t_ in range(NT):
              pbb = gbank().bitcast(BF16)
              for h in range(4):
                  tr(pbb[:, h * 128:(h + 1) * 128], kT[:, h, t_ * 128:(t_ + 1) * 128], ident_b)
              cp(ktok[:, t_, :, :], pbb[:, 0:512].rearrange("p (h d) -> p h d", h=4))

          for d_ in range(2):
              pu = psO
              for t_ in range(NT):
                  tr(pu[0:4, t_ * 128:(t_ + 1) * 128], uu[:, t_, d_ * 4:(d_ + 1) * 4], ident_f)
              red(umax[:, d_, :], pu[0:4, :].rearrange("p (c s) -> p c s", s=64), ALU.max)
              pv = psD
              for t_ in range(NT):
                  tr(pv[0:4, t_ * 128:(t_ + 1) * 128], spf[:, t_, d_ * 4:(d_ + 1) * 4], ident_f)
              red(blast[:, d_, :], pv[0:4, :].rearrange("p (c s) -> p c s", s=64), ALU.add)

          act(em0[:, :], em0[:, :], AF.Exp)
          memset(Cst[:, :, 129:130], 0.0)
          for i in range(8):
              ts(Cst[:, i, :], Cst[:, i, :], em0[:, i:i + 1], ALU.mult)
          cp(Cbs[0][:, :, :], Cst[:, :, :])
          cp(Cbs[1][:, :, :], Cst[:, :, :])

          if stop == "prescan":
              raise _Stop()
          P.phase = "scan"
          for d_ in range(2):
              order = list(range(NT)) if d_ == 0 else list(range(NT - 1, -1, -1))
              mcol = 0
              cbi = 0
              C4 = Cst[:, d_ * 4:(d_ + 1) * 4, 0:129]
              def emit_vb(step_):
                  tq = order[step_]
                  Bbc_ = gsc[:, tq, d_ * 4:d_ * 4 + 4].unsqueeze(2).to_broadcast([128, 4, 132])
                  tt(vB[step_ % 2][:, :, :], vaug[:, tq, :, :], Bbc_, ALU.mult, eng="pool")

              emit_vb(0)
              modw = {}
              for step, t_ in enumerate(order):
                  if step + 1 < NT:
                      emit_vb(step + 1)
                  if d_ == 0 and l + 1 < nlayers and step < 6:
                      modw[step] = mod_block_load(l + 1, step)
                  if d_ == 0 and l + 1 < nlayers and 1 <= step < 7:
                      mod_block_compute(l + 1, step - 1, modw.pop(step - 1))
                  if d_ == 1 and step == 5:
                      w_zc = load_w(D["W_in"][l][:, 5312:5824], 8, 512)
                  if d_ == 1 and step == 6:
                      w_mo = load_w(D["W_in"][l][:, 4800:5312], 8, 512)
                  pH = (psO if step % 2 == 0 else psA)[:, 0:512]
                  pSm = (psO if step % 2 == 0 else psA)[:, 512:1024]
                  pH4 = pH.rearrange("p (h c) -> p h c", h=4)
                  tsl = slice(t_ * 128, (t_ + 1) * 128)
                  vb = vB[step % 2]
                  pS = psB[:, 0:512]
                  for h in range(4):
                      mm(pS[:, h * 128:(h + 1) * 128], kT[:, h, tsl], qT[:, h, tsl], start=True, stop=True)
                  pm = ptm[step % 2]
                  tt(pm[:, :], pS[:, :], mask4[:, d_, :, :].rearrange("p h t -> p (h t)"), ALU.mult)
                  halves = (0, 1) if d_ == 0 else (1, 0)
                  pCs = {}
                  for hf in halves:
                      rs = slice(hf * 64, hf * 64 + 64)
                      pC = psD[:, hf * 512:(hf + 1) * 512]
                      pCs[hf] = pC
                      for h in range(4):
                          mm(pC[:, h * 128:(h + 1) * 128], ktok[rs, t_, h, :], vb[rs, h, 0:128], start=True, stop=True)
                  for h in range(4):
                      mm(pH4[:, h, :], pm[:, h * 128:(h + 1) * 128], vb[:, h, 0:128],
                         start=(h == 0), stop=False, skip=True)
                  for h in range(4):
                      mm(pSm[:, h:h + 1], pm[:, h * 128:(h + 1) * 128], vb[:, h, 130:131],
                         start=(h == 0), stop=False, skip=True)
                  for h in range(4):
                      mm(pSm[:, 8 + 2 * h:10 + 2 * h], ktok[:, t_, h, :], vb[:, h, 128:130],
                         start=False, stop=True, skip=True)
                  for hf in halves:
                      rs = slice(hf * 64, hf * 64 + 64)
                      qs_ = slice(t_ * 128 + hf * 64, t_ * 128 + hf * 64 + 64)
                      Cb = Cbs[cbi]
                      for h in range(4):
                          mm(pH4[rs, h, :], qT[:, h, qs_], Cb[:, d_ * 4 + h, 0:128],
                             start=False, stop=True, skip=True)
                      for h in range(4):
                          mm(pSm[rs, h:h + 1], qT[:, h, qs_], Cb[:, d_ * 4 + h, 128:129],
                             start=False, stop=True, skip=True)
                      a4 = alast[:, t_, hf * 8 + d_ * 4:hf * 8 + d_ * 4 + 4]
                      abc = a4.unsqueeze(2).to_broadcast([128, 4, 128])
                      pC4 = pCs[hf].rearrange("p (h c) -> p h c", h=4)
                      tt(C4[:, :, 0:128], pC4, C4[:, :, 0:128], ALU.add)
                      tt(C4[:, :, 0:128], C4[:, :, 0:128], abc, ALU.mult)
                      dn4 = pSm[:, 8:16].rearrange("p (h two) -> p h two", two=2)[:, :, hf]
                      tt(C4[:, :, 128], dn4, C4[:, :, 128], ALU.add)
                      tt(C4[:, :, 128], C4[:, :, 128], a4, ALU.mult)
                      c_orig = t_ * 2 + hf
                      cidx = c_orig if d_ == 0 else 15 - c_orig
                      prev = mch[:, d_, mcol:mcol + 1]
                      nxt = mch[:, d_, mcol + 1:mcol + 2]
                      tt(nxt, prev, umax[:, d_, c_orig:c_orig + 1], ALU.max)
                      tt(nxt, nxt, blast[:, d_, c_orig:c_orig + 1], ALU.subtract)
                      mcol += 1
                      if cidx % 4 == 3:
                          seg = c_orig // 4
                          cp(mout[:, seg * 2 + d_:seg * 2 + d_ + 1], nxt)
                          ts(mdiag[:, seg * 2 + d_, :], ident_f[0:4, 0:4], nxt, ALU.mult)
                          pe_ = psB[:, 512:1024]
                          mm(pe_[:, 0:4], ones_f[0:4, :], mdiag[:, seg * 2 + d_, :], start=True, stop=True)
                          ec = (seg * 2 + d_) * 4
                          act(emout[:, ec:ec + 4], pe_[:, 0:4], AF.Exp, scale=-1.0)
                          ebc = emout[:, ec:ec + 4].unsqueeze(2).to_broadcast([128, 4, 129])
                          tt(cout[:, 0, :, 0:129], C4, ebc, ALU.mult)
                          dma(D["o_C"][l, seg, d_].rearrange("h k v -> k h v"), cout[:, 0, :, 0:128])
                          dma(D["o_n"][l, seg, d_].rearrange("h (k o) -> k h o", o=1), cout[:, 0, :, 128:129])
                          if cidx != 15:
                              nseg = seg + 1 if d_ == 0 else seg - 1
                              kbc = keep[:, nseg:nseg + 1].unsqueeze(2).to_broadcast([128, 4, 129])
                              tt(C4, C4, kbc, ALU.mult)
                              nn = mch[:, d_, mcol + 1:mcol + 2]
                              tt(nn, nxt, keep[0:4, nseg:nseg + 1], ALU.mult)
                              mcol += 1
                      cbi = 1 - cbi
                      cp(Cbs[cbi][:, d_ * 4:(d_ + 1) * 4, 0:129], C4)
                  invE = gsc[:, t_, 16 + d_ * 4:16 + d_ * 4 + 4]
                  dt_ = dtmp[:, step % 2, :]
                  act(dt_[:, 0:4], pSm[:, 0:4], AF.Abs)
                  tt(dt_[:, 4:8], dt_[:, 0:4], invE, ALU.max)
                  recip(dt_[:, 12:16], dt_[:, 4:8])
                  h4 = hsum[:, t_, :].rearrange("p (h d) -> p h d", h=4)
                  if d_ == 0:
                      for h in range(4):
                          act(h4[:, h, :], pH4[:, h, :], AF.Copy, scale=dt_[:, 12 + h:13 + h])
                  else:
                      hb = ropet[step % 2][:, :].rearrange("p (h d) -> p h d", h=4)
                      for h in range(4):
                          act(hb[:, h, :], pH4[:, h, :], AF.Copy, scale=dt_[:, 12 + h:13 + h])
                      tt(h4, hb, h4, ALU.add, eng="pool")
          dma(D["o_m"][l].rearrange("s d (h o) -> h s d o", o=1), mout[:, :].rearrange("h (s d o) -> h s d o", d=2, o=1))

          if stop == "scan":
              raise _Stop()
          P.phase = "mlstm_out"
          def z_consumer(mi, ch, ps):
              act(zT[:, mi, ch * 512:(ch + 1) * 512], ps, AF.Silu)
          featmaj_proj(l, 5312, 512, z_consumer, w=w_zc)

          def mo_consumer(t_, ps):
              sg = toktmp()
              act(sg[:, :], ps, AF.Sigmoid)
              tt(sg[:, :], sg[:, :], hsum[:, t_, :], ALU.mult)
              junk = tmpB
              for h in range(4):
                  act(junk[:, 0:128], sg[:, h * 128:(h + 1) * 128], AF.Square, accum=ssm[:, h:h + 1])
              act(ssm[:, :], ssm[:, :], AF.Ln, bias=eps_c[:, :], scale=1.0 / 128)
              act(ssm[:, :], ssm[:, :], AF.Exp, scale=-0.5)
              ob = sqbuf()
              for h in range(4):
                  stt(ob[:, h * 128:(h + 1) * 128], sg[:, h * 128:(h + 1) * 128], ssm[:, h:h + 1],
                      mlnorm[:, 0, h * 128:(h + 1) * 128], ALU.mult, ALU.mult)
              def tail():
                  pbb = gbank().bitcast(BF16)
                  for h in range(4):
                      tr(pbb[:, h * 128:(h + 1) * 128], ob[:, h * 128:(h + 1) * 128], ident_b)
                  tt(yT[:, :, t_ * 128:(t_ + 1) * 128], pbb[:, 0:512].rearrange("p (h t) -> p h t", h=4),
                     zT[:, :, t_ * 128:(t_ + 1) * 128], ALU.mult)
              return tail
          tokmaj_proj(l, 4800, 512, mo_consumer, w=w_mo)
          out_part(l, 2)

          if stop == "mlstm":
              raise _Stop()
          P.phase = "mla_proj"
          memset(sh8[64:128, :], 0.0, eng="dve")
          dma(ctxf[:, :, 0:256], D["cckv"][l].rearrange("(t p) c -> p t c", p=128))
          dma(ctxf[:, :, 256:320], D["ckrope"][l].rearrange("(t p) c -> p t c", p=128))
          pq = [None]

          def cq_consumer(mi, ch, ps):
              if mi == 0:
                  pq[0] = xbank()
              ts(tmpcq[:, mi, ch * 512:(ch + 1) * 512], ps, gq[:, l, mi:mi + 1], ALU.mult)
              sq = sqbuf()
              act(sq[:, :], ps, AF.Square)
              pqb = pq[0]

              def tail():
                  mm(pqb[:, :], ones_b[:, :], sq[:, :], start=(mi == 0), stop=(mi == 2))
                  if mi == 2:
                      act(rstd[:, ch * 512:(ch + 1) * 512], pqb[:, :], AF.Ln, bias=eps_c[:, :], scale=1.0 / 384)
                      act(rstd[:, ch * 512:(ch + 1) * 512], rstd[:, ch * 512:(ch + 1) * 512], AF.Exp, scale=-0.5)
              return tail
          tmpcq = hsum[:, :, :].rearrange("p a b -> p (a b)")[:, 0:3 * T].rearrange("p (a b) -> p a b", a=3)
          featmaj_proj(l, 0, 384, cq_consumer)
          wuq = load_w(D["W_uq"][l], 3, 768)
          for ch in range(2):
              csl = slice(ch * 512, (ch + 1) * 512)
              for h in range(4):
                  pb = gbank()
                  for kt in range(3):
                      mm(pb[:, :], wuq[:, kt, h * 192:h * 192 + 128], tmpcq[:, kt, csl], start=(kt == 0), stop=(kt == 2))
                  tt(attQ[:, h, csl], pb[:, :], rstd[:, csl], ALU.mult)
                  pb = gbank()
                  for kt in range(3):
                      mm(pb[0:64, :], wuq[:, kt, h * 192 + 128:h * 192 + 192], tmpcq[:, kt, csl],
                         start=(kt == 0), stop=(kt == 2))
                  tt(qropeT[:, h, csl], pb[0:64, :], rstd[0:64, csl], ALU.mult)

          for ch in range(2):
              for h in range(4):
                  csl = slice(ch * 512, (ch + 1) * 512)
                  rope(qropeT[:, h, csl], qropeT[:, h, csl], 64, ch * 512, 512)
          def kv_consumer(t_, ps):
              tsl = slice(t_ * 128, (t_ + 1) * 128)
              junk = tmpB
              act(junk[:, 0:256], ps[:, 0:256], AF.Square, accum=ssk[:, t_:t_ + 1])
              act(ssk[:, t_:t_ + 1], ssk[:, t_:t_ + 1], AF.Ln, bias=eps_c[:, :], scale=1.0 / 256)
              act(ssk[:, t_:t_ + 1], ssk[:, t_:t_ + 1], AF.Exp, scale=-0.5)
              tk = toktmp()
              stt(tk[:, 0:256], ps[:, 0:256], ssk[:, t_:t_ + 1], gkv[:, 0, :], ALU.mult, ALU.mult)
              cp(tk[:, 256:320], ps[:, 256:320], eng="act")
              dma(D["o_ckv"][l, tsl, :], tk[:, 0:256])
              dma(D["o_krope"][l, tsl, :], tk[:, 256:320])
              def tail():
                  pb = gbank()
                  for kt in range(2):
                      tr(pb[:, kt * 128:(kt + 1) * 128], tk[:, kt * 128:(kt + 1) * 128], ident_f)
                  tr(pb[0:64, 256:384], tk[:, 256:320], ident_f)
                  cp(ckvnT[:, :, tsl], pb[:, 0:256].rearrange("p (k t) -> p k t", k=2))
                  cp(krraw[0:64, tsl], pb[0:64, 256:384], eng="act")
              return tail
          tokmaj_proj(l, 384, 320, kv_consumer)
          for ch in range(2):
              rope(kropeT[:, ch * 512:(ch + 1) * 512], krraw[0:64, ch * 512:(ch + 1) * 512], 64, ch * 512, 512)
          for t2 in range(2):
              pb = gbank()
              for kt in range(2):
                  tr(pb[:, kt * 128:(kt + 1) * 128], ctxf[:, t2, kt * 128:(kt + 1) * 128], ident_f)
              tr(pb[0:64, 256:384], ctxf[:, t2, 256:320], ident_f)
              ksl = slice(T + t2 * 128, T + (t2 + 1) * 128)
              cp(ckvnT[:, :, ksl], pb[:, 0:256].rearrange("p (k t) -> p k t", k=2))
              cp(kropeT[:, ksl], pb[0:64, 256:384], eng="act")
          wkv = load_w(D["W_ukv"][l], 2, 1024)
          wkv5 = wkv.rearrange("p k (h w d) -> p k h w d", h=4, w=2)
          for h in range(4):
              for (k0, kn) in ((0, 512), (512, 512), (1024, 256)):
                  pb = gbank()
                  for kt in range(2):
                      mm(pb[:, 0:kn], wkv5[:, kt, h, 0, :], ckvnT[:, kt, k0:k0 + kn], start=(kt == 0), stop=(kt == 1))
                  cp(attK[:, h, k0:k0 + kn], pb[:, 0:kn], eng="act")
          for kt_ in range(NKT):
              pb = gbank()
              for kt in range(2):
                  mm(pb[:, :].rearrange("p (h d) -> p h d", h=4), ckvnT[:, kt, kt_ * 128:(kt_ + 1) * 128],
                     wkv5[:, kt, :, 1, :], start=(kt == 0), stop=(kt == 1))
              cp(attV[:, kt_, :], pb[:, :])
          featmaj_proj(l, 704, 512, z_consumer)

          P.phase = "mla_att"
          def mla_scores(u, kt, qs, out, first):
              mm(out, attK[:, u, kt * 128:(kt + 1) * 128], attQ[:, u, qs * 256:(qs + 1) * 256],
                 start=first, stop=False, skip=True)
              mm(out, kropeT_full[:, kt * 128:(kt + 1) * 128], qrope128[:, u, qs * 256:(qs + 1) * 256],
                 start=False, stop=True, skip=True)

          def mla_post(qs, o):
              tt(yT[:, :, qs * 256:(qs + 1) * 256], o[:, :].rearrange("p (u q) -> p u q", u=4),
                 zT[:, :, qs * 256:(qs + 1) * 256], ALU.mult)
          attention(mla_scores, lambda u, kt: attV[:, kt, u * 128:(u + 1) * 128], MLA_SCALE, mla_post)
          out_part(l, 0)

          if stop == "mla":
              raise _Stop()
          P.phase = "diff_proj"
          lam_init = 0.8 - 0.6 * math.exp(-0.3 * l)
          junk = tmpB
          stt(junk[:, 0:64], dlam[:, 0, 0:64], 1.0, dlam[:, 0, 64:128], ALU.mult, ALU.mult)
          red(nlam[:, 0:1], junk[:, 0:64], ALU.add)
          stt(junk[:, 64:128], dlam[:, 0, 128:192], 1.0, dlam[:, 0, 192:256], ALU.mult, ALU.mult)
          red(nlam[:, 1:2], junk[:, 64:128], ALU.add)
          act(nlam[:, 0:2], nlam[:, 0:2], AF.Exp)
          tt(nlam[:, 2:3], nlam[:, 1:2], nlam[:, 0:1], ALU.subtract)
          ts(nlam[:, 3:4], nlam[:, 2:3], -lam_init, ALU.add)
          ts(gdl[:, :], dnorm[:, l:l + 1], 1.0 - lam_init, ALU.mult)

          dma(ctxf[:, :, :], D["cdk"][l].rearrange("(t p) c -> p t c", p=128))
          q1v = sh8[:, :].rearrange("p (a b) -> p a b", a=4)
          memset(sh8[0:64, :], 0.0, eng="dve")

          def dq_consumer(mi, ch, ps):
              cp(attQ[:, mi, ch * 512:(ch + 1) * 512], ps, eng="act")
          featmaj_proj(l, 1216, 512, dq_consumer)

          def dk_consumer(mi, ch, ps):
              cp(attK[:, mi, ch * 512:(ch + 1) * 512], ps, eng="act")
          featmaj_proj(l, 1728, 512, dk_consumer)
          for ch in range(2):
              for mi in range(4):
                  csl = slice(ch * 512, (ch + 1) * 512)
                  rope(attQ[:, mi, csl], attQ[:, mi, csl], 128, ch * 512, 512, out_hi=q1v[64:128, mi, csl])
                  rope(attK[:, mi, csl], attK[:, mi, csl], 128, ch * 512, 512)
          memset(attQ[64:128, :, :], 0.0, eng="dve")

          def dk_tok_consumer(t_, ps):
              tk = toktmp()
              cp(tk[:, :], ps, eng="act")
              dma(D["o_dk"][l, t_ * 128:(t_ + 1) * 128, :], tk[:, :])
          tokmaj_proj(l, 1728, 512, dk_tok_consumer)

          def dv_consumer(t_, ps):
              tk = toktmp()
              cp(tk[:, :], ps, eng="act")
              dma(D["o_dv"][l, t_ * 128:(t_ + 1) * 128, :], tk[:, :])
              cp(attV[:, t_, :], ps)
          tokmaj_proj(l, 2240, 512, dv_consumer)
          dma(attV[:, 8:10, :], D["cdv"][l].rearrange("(t p) c -> p t c", p=128), eng="pool")
          for t2 in range(2):
              pb = gbank()
              for h in range(4):
                  tr(pb[:, h * 128:(h + 1) * 128], ctxf[:, t2, h * 128:(h + 1) * 128], ident_f)
              cp(attK[:, :, T + t2 * 128:T + (t2 + 1) * 128], pb[:, :].rearrange("p (h t) -> p h t", h=4))
          featmaj_proj(l, 2752, 512, z_consumer)
          P.phase = "diff_att"
          for pair in range(2):
              def d_scores(u, kt, qs, out, first, pair=pair):
                  h = pair * 2 + u % 2
                  m_ = u // 2
                  qsrc = attQ if m_ == 0 else q1v
                  mm(out, attK[:, h, kt * 128:(kt + 1) * 128],
                     qsrc[:, h, qs * 256:(qs + 1) * 256], start=first, stop=True, skip=True)

              def d_post(qs, o, pair=pair):
                  o4 = o[:, :].rearrange("p (m h q) -> p m h q", h=2, m=2)
                  ddb = ddbuf[qs % 2]
                  dd = ddb[:, :].rearrange("p (h q) -> p h q", h=2)
                  stt(dd, o4[:, 1, :, :], nlam[:, 3:4], o4[:, 0, :, :], ALU.mult, ALU.add)
                  sq = sqbuf()
                  tt(sq[:, :], ddb[:, :], ddb[:, :], ALU.mult)

                  def stage2(pb):
                      mm(pb[:, :], ones_b[:, :], sq[:, :], start=True, stop=True)
                      rs_ = tmpB[:, 512:1024]
                      act(rs_, pb[:, :], AF.Ln, bias=eps_c[:, :], scale=1.0 / 128)
                      act(rs_, rs_, AF.Exp, scale=-0.5)
                      stt(rs_, ddb[:, :], gdl[:, 0:1], rs_, ALU.mult, ALU.mult)
                      tt(yT[:, pair * 2:pair * 2 + 2, qs * 256:(qs + 1) * 256], rs_.rearrange("p (h q) -> p h q", h=2),
                         zT[:, pair * 2:pair * 2 + 2, qs * 256:(qs + 1) * 256], ALU.mult)
                  return stage2
              attention(d_scores, lambda u, kt, pair=pair: attV[:, kt, (pair * 2 + u % 2) * 128:(pair * 2 + u % 2 + 1) * 128],
                        DIFF_SCALE, d_post)
          out_part(l, 1)

    except _Stop:
        pass
    P.phase = "final"
    for ch in range(2):
        pb = xbank()
        for kt in range(8):
            sq = sqbuf()
            act(sq[:, :], xT[:, kt, ch * 512:(ch + 1) * 512], AF.Square)
            mm(pb[:, :], ones_b[:, :], sq[:, :], start=(kt == 0), stop=(kt == 7))
        act(rstd[:, ch * 512:(ch + 1) * 512], pb[:, :], AF.Ln, bias=eps_c[:, :], scale=1.0 / 1024)
        act(rstd[:, ch * 512:(ch + 1) * 512], rstd[:, ch * 512:(ch + 1) * 512], AF.Exp, scale=-0.5)
    for kt in range(8):
        stt(xT[:, kt, :], xT[:, kt, :], gfin[:, kt:kt + 1], rstd[:, :], ALU.mult, ALU.mult)
    for t_ in range(NT):
        xo = xin[t_ % 2]
        for half in range(2):
            pb = gbank()
            for j in range(4):
                kt = half * 4 + j
                tr(pb[:, j * 128:(j + 1) * 128], xT[:, kt, t_ * 128:(t_ + 1) * 128], ident_f)
            cp(xo[:, half * 512:(half + 1) * 512], pb[:, :], eng="act" if half else "dve")
        dma(D["y"][t_ * 128:(t_ + 1) * 128, :], xo[:, :])
    P.emit()
    global _LASTP
    _LASTP = P
    import os
    if os.environ.get("KDBG"):
        import json
        json.dump([(o.eng, ph) for o, ph in zip(P.ops, P.phases)], open(os.environ["KDBG"], "w"))
    return nc


def _consts():
    c = np.zeros((128, 6, 128), np.float32)
    i = np.arange(128)
    c[:, 0, :] = np.eye(128)
    part = (i + 32) % 64 + 64 * (i // 64)
    c[i, 1, part] = 1.0
    same = (i[:, None] // 64) == (i[None, :] // 64)
    c[:, 2, :] = (same & (i[:, None] <= i[None, :])).astype(np.float32)
    c[:, 3, :] = (same & (i[:, None] >= i[None, :])).astype(np.float32)
    c[:, 4, :] = (i[:, None] < 64).astype(np.float32) * np.ones((1, 128), np.float32)
    c[:, 5, :] = (i[:, None] >= 64).astype(np.float32) * np.ones((1, 128), np.float32)
    return c


def _rope_tables(identity):
    if identity:
        return np.ones((128, T), np.float32), np.zeros((128, T), np.float32)
    n_freq = 16
    inv = 10000.0 ** (-np.arange(n_freq, dtype=np.float64) / n_freq)
    t = np.arange(T)
    row = (t // 64).astype(np.float64)
    col = (t % 64).astype(np.float64)
    ang = np.concatenate([row[:, None] * inv, col[:, None] * inv], axis=-1)
    c = np.cos(ang).T.astype(np.float32)
    s = np.sin(ang).T.astype(np.float32)
    cos2 = np.concatenate([c, c, c, c], 0)
    sin2 = np.concatenate([-s, s, -s, s], 0)
    return np.ascontiguousarray(cos2), np.ascontiguousarray(sin2)


_NC_CACHE = {}


def kernel(x_prompt, x_sample, cache_mla_ckv, cache_mla_krope, cache_diff_k, cache_diff_v,
           state_mlstm_C, state_mlstm_n, state_mlstm_m, c, c_ctx, g_norm, W_mod, b_mod, W_in,
           mla_q_norm, W_uq, mla_kv_norm, W_ukv, diff_lambda, diff_norm, ml_conv, ml_gate_b,
           ml_norm, W_out, g_final):
    f = lambda a: np.ascontiguousarray(np.asarray(a, dtype=np.float32))
    shared = {"g_norm": f(g_norm), "W_mod": f(W_mod), "b_mod": f(b_mod), "W_in": f(W_in),
              "mla_q_norm": f(mla_q_norm), "W_uq": f(W_uq), "mla_kv_norm": f(mla_kv_norm),
              "W_ukv": f(W_ukv), "diff_lambda": f(diff_lambda).reshape(L, 256), "diff_norm": f(diff_norm),
              "ml_conv": f(ml_conv), "ml_gate_b": f(ml_gate_b).reshape(L, 16),
              "ml_norm": f(ml_norm).reshape(L, 512), "W_out": f(W_out), "g_final": f(g_final),
              "consts": _consts()}
    x_prompt, x_sample = f(x_prompt), f(x_sample)
    in_maps = []
    for core in range(8):
        m = dict(shared)
        if core < 4:
            b = core
            m["x"] = x_sample[b]
            m["cvec"] = f(c)[b]
            m["cckv"] = f(cache_mla_ckv)[b]
            m["ckrope"] = f(cache_mla_krope)[b]
            m["cdk"] = f(cache_diff_k)[b].reshape(L, 256, 512)
            m["cdv"] = f(cache_diff_v)[b].reshape(L, 256, 512)
            m["sC"] = f(state_mlstm_C)[b]
            m["sn"] = f(state_mlstm_n)[b]
            m["sm"] = f(state_mlstm_m)[b].reshape(L, 8)
            m["maskb"] = np.zeros((128, 40), np.float32)
            m["keep"] = np.ones((128, 4), np.float32)
            m["kbar"] = np.zeros((128, 1), np.float32)
            m["cos2"], m["sin2"] = _rope_tables(False)
        else:
            b0 = (core - 4) * 4
            m["x"] = np.ascontiguousarray(x_prompt[b0:b0 + 4].reshape(T, 1024))
            m["cvec"] = f(c_ctx)
            m["cckv"] = np.zeros((L, 256, 256), np.float32)
            m["ckrope"] = np.zeros((L, 256, 64), np.float32)
            m["cdk"] = np.zeros((L, 256, 512), np.float32)
            m["cdv"] = np.zeros((L, 256, 512), np.float32)
            m["sC"] = np.zeros((L, 2, 4, 128, 128), np.float32)
            m["sn"] = np.zeros((L, 2, 4, 128), np.float32)
            m["sm"] = np.zeros((L, 8), np.float32)
            mb = np.full((128, NKT, 4), NEG, np.float32)
            for kt in range(8):
                mb[:, kt, kt // 2] = 0.0
            m["maskb"] = mb.reshape(128, 40)
            m["keep"] = np.zeros((128, 4), np.float32)
            m["kbar"] = np.ones((128, 1), np.float32)
            m["cos2"], m["sin2"] = _rope_tables(True)
        in_maps.append({k: np.ascontiguousarray(v) for k, v in m.items()})
    if "nc" not in _NC_CACHE:
        _NC_CACHE["nc"] = build()
    res = run_bass_kernel_spmd(_NC_CACHE["nc"], in_maps, core_ids=list(range(8)))
    R = res.results
    y_sample = np.stack([R[b]["y"] for b in range(4)], 0).astype(np.float32)
    y_prompt = np.zeros((16, 256, 1024), np.float32)
    new_ckv = np.zeros((16, L, 256, 256), np.float32)
    new_krope = np.zeros((16, L, 256, 64), np.float32)
    new_dk = np.zeros((16, L, 256, 4, 128), np.float32)
    new_dv = np.zeros((16, L, 256, 4, 128), np.float32)
    new_C = np.zeros((16, L, 2, 4, 128, 128), np.float32)
    new_n = np.zeros((16, L, 2, 4, 128), np.float32)
    new_m = np.zeros((16, L, 2, 4), np.float32)
    for core in range(4, 8):
        r = R[core]
        for s in range(4):
            b = (core - 4) * 4 + s
            sl = slice(s * 256, (s + 1) * 256)
            y_prompt[b] = r["y"][sl]
            new_ckv[b] = r["o_ckv"][:, sl, :]
            new_krope[b] = r["o_krope"][:, sl, :]
            new_dk[b] = r["o_dk"][:, sl, :].reshape(L, 256, 4, 128)
            new_dv[b] = r["o_dv"][:, sl, :].reshape(L, 256, 4, 128)
            new_C[b] = r["o_C"][:, s]
            new_n[b] = r["o_n"][:, s]
            new_m[b] = r["o_m"][:, s]
    return (y_prompt, y_sample, new_ckv, new_krope, new_dk, new_dv, new_C, new_n, new_m)
```

```python
import math
import contextlib
import numpy as np
import concourse.bass as bass
import concourse.mybir as mybir
from concourse.bass_utils import run_bass_kernel_spmd

F32 = mybir.dt.float32
BF16 = mybir.dt.bfloat16
AF = mybir.ActivationFunctionType
ALU = mybir.AluOpType
AX = mybir.AxisListType

ENGS = ("pe", "act", "dve", "pool", "sp")
STRICT = True


def _region(ap):
    t = ap.tensor
    es = mybir.dt.size(ap.dtype)
    pat = list(ap.ap)
    off = int(ap.offset)
    if "DRAM" in str(ap.space).upper():
        lo = off
        hi = off + sum((c - 1) * abs(s) for s, c in pat) + 1
        return (t.name, 0, 1, lo * es, hi * es)
    pcnt = pat[0][1]
    per_part = 1
    for d in list(t.shape)[1:]:
        per_part *= int(d)
    p0 = off // per_part
    f0 = off % per_part
    ext = sum((c - 1) * abs(s) for s, c in pat[1:]) + 1
    if "PSUM" in str(ap.space).upper():
        b0 = (f0 * es) // 2048 * 2048
        b1 = -(-((f0 + ext) * es) // 2048) * 2048
        return (t.name, p0 // 32 * 32, -(-(p0 + pcnt) // 32) * 32, b0, b1, True)
    return (t.name, p0, p0 + pcnt, f0 * es, (f0 + ext) * es)


def _overlap(a, b):
    return a[0] == b[0] and a[1] < b[2] and b[1] < a[2] and a[3] < b[4] and b[3] < a[4]


def _covers(a, b):
    return a[0] == b[0] and a[1] <= b[1] and a[2] >= b[2] and a[3] <= b[3] and a[4] >= b[4]


class Op:
    __slots__ = ("eng", "fn", "reads", "writes", "deps", "idx", "sig", "dma")

    def __init__(self, eng, fn, reads, writes, dma=False):
        self.eng, self.fn, self.dma = eng, fn, dma
        self.reads = [_region(a) for a in reads if a is not None]
        self.writes = [_region(a) for a in writes if a is not None]
        self.deps = set()
        self.sig = 0


class Prog:
    NROT = 24

    def __init__(self, nc):
        self.nc = nc
        self.ops = []
        self.acc = {}
        self.phase = "init"
        self.phases = []

    def op(self, eng, fn, reads=(), writes=(), dma=False):
        o = Op(eng, fn, reads, writes, dma)
        o.idx = len(self.ops)
        self.phases.append(self.phase)
        self.ops.append(o)
        for r in o.reads:
            ps = len(r) > 5
            for (reg, oi, w, en) in self.acc.setdefault(r[0], []):
                if (w or (ps and en != eng)) and _overlap(reg, r):
                    o.deps.add(oi)
        for r in o.writes:
            for (reg, oi, w, en) in self.acc.setdefault(r[0], []):
                if _overlap(reg, r):
                    o.deps.add(oi)
        for r in o.writes:
            lst = self.acc[r[0]]
            lst[:] = [e for e in lst if not _covers(r, e[0])]
            lst.append((r, o.idx, True, eng))
        for r in o.reads:
            self.acc[r[0]].append((r, o.idx, False, eng))
        o.deps.discard(o.idx)
        return o

    def emit(self):
        nc, ops, NROT = self.nc, self.ops, self.NROT
        need = [None] * len(ops)
        signal = [False] * len(ops)
        for o in ops:
            best = {}
            for d in o.deps:
                p = ops[d]
                if p.dma:
                    best[("dma", d)] = d
                    continue
                if p.eng == o.eng:
                    if o.eng in ("pe", "sp") or o.dma:
                        continue
                    if not STRICT and not any(_overlap(w, r) for w in p.writes for r in o.reads):
                        continue
                if best.get(p.eng, -1) < d:
                    best[p.eng] = d
            need[o.idx] = list(best.values())
            for d in need[o.idx]:
                signal[d] = True
        cnt = {e: 0 for e in ENGS}
        dcnt = {e: 0 for e in ENGS}
        for o in ops:
            if o.dma:
                o.sig = dcnt[o.eng]
                dcnt[o.eng] += 1
            elif signal[o.idx]:
                cnt[o.eng] += 1
                o.sig = cnt[o.eng]
        per_eng = {e: [] for e in ENGS}
        for o in ops:
            per_eng[o.eng].append(o)
        with contextlib.ExitStack() as st:
            sems = {e: st.enter_context(nc.semaphore("s_" + e)) for e in ENGS}
            dsems = {}
            for e in ENGS:
                if dcnt[e] > 0:
                    dsems[e] = [st.enter_context(nc.semaphore("d_%s_%d" % (e, i)))
                                for i in range(min(NROT, dcnt[e]))]
            block = st.enter_context(nc.Block())
            engobj = {"pe": "tensor", "act": "scalar", "dve": "vector", "pool": "gpsimd", "sp": "sync"}

            def run(ename, e):
                known = {}

                def wait(key, s, v):
                    if known.get(key, 0) >= v:
                        return
                    known[key] = v
                    e.wait_ge(s, v)

                for o in per_eng[ename]:
                    for d in need[o.idx]:
                        p = ops[d]
                        if p.dma:
                            slot = p.sig % NROT
                            wait((p.eng, "d", slot), dsems[p.eng][slot], 16 * (p.sig // NROT + 1))
                        else:
                            wait((p.eng, "c"), sems[p.eng], p.sig)
                    if o.dma and o.sig >= NROT:
                        slot = o.sig % NROT
                        wait((ename, "d", slot), dsems[ename][slot], 16 * (o.sig // NROT))
                    ins = o.fn(e)
                    if o.dma:
                        ins.then_inc(dsems[ename][o.sig % NROT], 16)
                    elif signal[o.idx]:
                        ins.then_inc(sems[ename], 1)
                n = dcnt[ename]
                for slot in range(min(NROT, n)):
                    last = ((n - 1 - slot) // NROT) * NROT + slot
                    wait((ename, "d", slot), dsems[ename][slot], 16 * (last // NROT + 1))

            for ename in ENGS:
                if not per_eng[ename]:
                    continue

                def body(e, ename=ename):
                    run(ename, e)
                getattr(block, engobj[ename])(body)


T = 1024
NT = 8
L = 4
NKT = 10
TK = 1280
EPS = 1e-6
MLA_SCALE = 192 ** -0.5
DIFF_SCALE = 64 ** -0.5
NEG = -30000.0

W_SPECS = [("g_norm", [L, 1024]), ("W_mod", [L, 1024, 3072]), ("b_mod", [L, 3072]),
           ("W_in", [L, 1024, 5840]), ("mla_q_norm", [L, 384]), ("W_uq", [L, 384, 768]),
           ("mla_kv_norm", [L, 256]), ("W_ukv", [L, 256, 1024]), ("diff_lambda", [L, 256]),
           ("diff_norm", [L, 128]), ("ml_conv", [L, 3, 1024]), ("ml_gate_b", [L, 16]),
           ("ml_norm", [L, 512]), ("W_out", [L, 1536, 1024]), ("g_final", [1024])]
IN_SPECS = [("x", [T, 1024]), ("cvec", [1024]), ("cckv", [L, 256, 256]), ("ckrope", [L, 256, 64]),
            ("cdk", [L, 256, 512]), ("cdv", [L, 256, 512]), ("sC", [L, 2, 4, 128, 128]),
            ("sn", [L, 2, 4, 128]), ("sm", [L, 8]), ("maskb", [128, 40]), ("keep", [128, 4]),
            ("kbar", [128, 1]), ("cos2", [128, T]), ("sin2", [128, T]), ("consts", [128, 6, 128])]
OUT_SPECS = [("y", [T, 1024]), ("o_ckv", [L, T, 256]), ("o_krope", [L, T, 64]), ("o_dk", [L, T, 512]),
             ("o_dv", [L, T, 512]), ("o_C", [L, 4, 2, 4, 128, 128]), ("o_n", [L, 4, 2, 4, 128]),
             ("o_m", [L, 4, 2, 4])]


class _Stop(Exception):
    pass


def build(nlayers=L, stop=None):
    nc = bass.Bass("TRN2", target_bir_lowering=False)
    P = Prog(nc)
    D = {}
    for n, s in W_SPECS + IN_SPECS:
        D[n] = nc.dram_tensor(n, s, F32, kind="ExternalInput").ap()
    for n, s in OUT_SPECS:
        D[n] = nc.dram_tensor(n, s, F32, kind="ExternalOutput").ap()

    def sb(name, shape, dt=F32):
        return nc.alloc_sbuf_tensor("s_" + name, shape, dt)

    def isap(v):
        return v is not None and not isinstance(v, (int, float))

    def mm(out, lhsT, rhs, start=True, stop=True, skip=False):
        P.op("pe", lambda e: e.matmul(out, lhsT=lhsT, rhs=rhs, start=start, stop=stop,
                                      skip_group_check=skip),
             [lhsT, rhs], [out])

    def tr(out, in_, ident):
        P.op("pe", lambda e: e.transpose(out, in_, ident), [in_, ident], [out])

    def act(out, in_, func, bias=None, scale=None, accum=None):
        kw = {}
        reads = [in_]
        if bias is not None:
            kw["bias"] = bias
            if isap(bias):
                reads.append(bias)
        if scale is not None:
            kw["scale"] = scale
            if isap(scale):
                reads.append(scale)
        if accum is not None:
            kw["accum_out"] = accum
        P.op("act", lambda e: e.activation(out=out, in_=in_, func=func, **kw), reads,
             [out] + ([accum] if accum is not None else []))

    def tt(out, a, b, op, eng="dve"):
        P.op(eng, lambda e: e.tensor_tensor(out=out, in0=a, in1=b, op=op), [a, b], [out])

    def ts(out, a, s1, op0, s2=None, op1=None, eng="dve"):
        reads = [a] + [s for s in (s1, s2) if isap(s)]
        if op1 is None:
            P.op(eng, lambda e: e.tensor_scalar(out=out, in0=a, scalar1=s1, scalar2=None, op0=op0), reads, [out])
        else:
            P.op(eng, lambda e: e.tensor_scalar(out=out, in0=a, scalar1=s1, scalar2=s2, op0=op0, op1=op1),
                 reads, [out])

    def stt(out, a, s, b, op0, op1):
        reads = [a, b] + ([s] if isap(s) else [])
        P.op("dve", lambda e: e.scalar_tensor_tensor(out=out, in0=a, scalar=s, in1=b, op0=op0, op1=op1),
             reads, [out])

    def cp(out, in_, eng="dve"):
        if eng == "act":
            P.op("act", lambda e: e.copy(out=out, in_=in_), [in_], [out])
        else:
            P.op(eng, lambda e: e.tensor_copy(out=out, in_=in_), [in_], [out])

    def red(out, in_, op):
        P.op("dve", lambda e: e.tensor_reduce(out=out, in_=in_, axis=AX.X, op=op), [in_], [out])

    def recip(out, in_):
        P.op("dve", lambda e: e.reciprocal(out=out, in_=in_), [in_], [out])

    def memset(ap, v, eng="pool"):
        P.op(eng, lambda e: e.memset(ap, v), [], [ap])

    def dma(out, in_, eng="sp"):
        P.op(eng, lambda e: e.dma_start(out=out, in_=in_, allow_slow_non_contiguous=True), [in_], [out], dma=True)

    def rsqrt_inplace(ap, scale, n=None):
        act(ap, ap, AF.Ln, bias=eps_c[0:ap.shape[0], :], scale=scale)
        act(ap, ap, AF.Exp, scale=-0.5)

    psA = nc.alloc_psum_tensor("psA", [128, 1024], F32)
    psB = nc.alloc_psum_tensor("psB", [128, 1024], F32)
    psO = nc.alloc_psum_tensor("psO", [128, 1024], F32)
    psD = nc.alloc_psum_tensor("psD", [128, 1024], F32)
    banks = [psA[:, 0:512], psA[:, 512:1024], psB[:, 0:512], psB[:, 512:1024], psO[:, 0:512], psO[:, 512:1024]]
    xbanks = [psD[:, 0:512], psD[:, 512:1024]]
    bctr = [0, 0]

    def gbank():
        b = banks[bctr[0] % 6]
        bctr[0] += 1
        return b

    def xbank():
        b = xbanks[bctr[1] % 2]
        bctr[1] += 1
        return b

    cst_f = sb("cst_f", [128, 6, 128])
    cst_b = sb("cst_b", [128, 6, 128], BF16)
    dma(cst_f[:, :, :], D["consts"][:, :, :])
    cp(cst_b[:, :, :], cst_f[:, :, :])
    ident_f, ident_b = cst_f[:, 0, :], cst_b[:, 0, :]
    pswap_b = cst_b[:, 1, :]
    maskF_f, maskB_f = cst_f[:, 2, :], cst_f[:, 3, :]
    selA_f, selB_f = cst_f[:, 4, :], cst_f[:, 5, :]
    ones_b = sb("ones_b", [128, 128], BF16)
    memset(ones_b[:, :], 1.0)
    ones_f = sb("ones_f", [128, 128])
    memset(ones_f[:, :], 1.0)
    eps_c = sb("eps_c", [128, 1])
    memset(eps_c[:, :], EPS)
    one_c = sb("one_c", [128, 1])
    memset(one_c[:, :], 1.0)
    lnq_c = sb("lnq_c", [128, 1])
    memset(lnq_c[:, :], math.log(128 ** -0.5))
    nlnq_c = sb("nlnq_c", [128, 1])
    memset(nlnq_c[:, :], -math.log(128 ** -0.5))
    mask4 = sb("mask4", [128, 2, 4, 128], BF16)
    for h in range(4):
        cp(mask4[:, 0, h, :], maskF_f)
        cp(mask4[:, 1, h, :], maskB_f)
    cos2 = sb("cos2", [128, T], BF16)
    sin2 = sb("sin2", [128, T], BF16)
    dma(cos2[:, :], D["cos2"][:, :], eng="pool")
    dma(sin2[:, :], D["sin2"][:, :], eng="pool")
    maskb = sb("maskb", [128, 40])
    dma(maskb[:, :], D["maskb"][:, :])
    keep = sb("keep", [128, 4])
    dma(keep[:, :], D["keep"][:, :])
    kbar = sb("kbar", [128, 1])
    dma(kbar[:, :], D["kbar"][:, :])

    gnorm = sb("gnorm", [128, L, 8])
    bmod = sb("bmod", [128, L, 24])
    gq = sb("gq", [128, L, 3])
    gkv = sb("gkv", [128, 1, 256])
    dlam = sb("dlam", [128, 1, 256])
    dnorm = sb("dnorm", [128, L])
    wconv = sb("wconv", [128, L, 3, 8])
    gateb = sb("gateb", [128, L, 16])
    mlnorm = sb("mlnorm", [128, 1, 512])
    gfin = sb("gfin", [128, 8])
    for l in range(L):
        dma(gnorm[:, l, :], D["g_norm"][l].rearrange("(kt p) -> p kt", p=128))
        dma(bmod[:, l, :], D["b_mod"][l].rearrange("(kt p) -> p kt", p=128))
        dma(gq[:, l, :], D["mla_q_norm"][l].rearrange("(kt p) -> p kt", p=128))
        dma(dnorm[:, l:l + 1], D["diff_norm"][l].rearrange("(p o) -> p o", o=1))
        for k in range(3):
            dma(wconv[:, l, k, :], D["ml_conv"][l, k].rearrange("(ct p) -> p ct", p=128))
    dma(gateb[:, :, :], D["ml_gate_b"].rearrange("(o l) c -> o l c", o=1).partition_broadcast(128))
    dma(gfin[:, :], D["g_final"].rearrange("(kt p) -> p kt", p=128))

    xT = sb("xT", [128, 8, T])
    hT = sb("hT", [128, 8, T], BF16)
    zT = sb("zT", [128, 4, T], BF16)
    yT = zT
    wbufs = [sb("wbuf%d" % i, [128, 8, 512], BF16) for i in range(3)]
    wctr = [0]
    attQ = sb("attQ", [128, 4, T], BF16)
    attK = sb("attK", [128, 4, TK], BF16)
    attV = sb("attV", [128, NKT, 512], BF16)
    sh8 = sb("sh8", [128, 4 * T], BF16)
    qropeT = sh8[0:64, :].rearrange("p (a b) -> p a b", a=4)
    qrope128 = sh8[:, :].rearrange("p (a b) -> p a b", a=4)
    kropeT_full = sb("kropeT", [128, TK], BF16)
    kropeT = kropeT_full[0:64, :]
    memset(kropeT_full[64:128, :], 0.0)
    ckvnT = sb("ckvnT", [128, 2, TK], BF16)
    rstd = sb("rstd", [128, T])
    tmpA = sb("tmpA", [128, T])
    tmpB = sb("tmpB", [128, T])
    osb = sb("osb", [128, T])
    ddbuf = [tmpB[:, 0:512], sb("ddb1", [128, 512])]
    ptb = [sb("pt%d" % i, [128, 1024], BF16) for i in range(2)]
    sqb = [sb("sq%d" % i, [128, 512], BF16) for i in range(2)]
    sqc = [0]

    def sqbuf():
        b = sqb[sqc[0] % 2]
        sqc[0] += 1
        return b

    xin = [tmpA, tmpB]
    for t_ in range(NT):
        xi = xin[t_ % 2]
        dma(xi[:, :], D["x"][t_ * 128:(t_ + 1) * 128, :])
        for half in range(2):
            pb = gbank()
            for j in range(4):
                kt = half * 4 + j
                tr(pb[:, j * 128:(j + 1) * 128], xi[:, kt * 128:(kt + 1) * 128], ident_f)
            for j in range(4):
                kt = half * 4 + j
                cp(xT[:, kt, t_ * 128:(t_ + 1) * 128], pb[:, j * 128:(j + 1) * 128],
                   eng="act" if j % 2 else "dve")

    def load_w(src2d, nk, ncols):
        wb = wbufs[wctr[0] % 3]
        wctr[0] += 1
        flat = wb[:, :, :].rearrange("p a b -> p (a b)")[:, 0:nk * ncols]
        view = flat.rearrange("p (a b) -> p a b", a=nk)
        dma(view, src2d.rearrange("(kt p) c -> p kt c", p=128), eng="pool")
        return view

    cv = sb("cv", [128, 8])
    dma(cv[:, :], D["cvec"].rearrange("(kt p) -> p kt", p=128))
    cs_b = sb("cs_b", [128, 8], BF16)
    act(cs_b[:, :], cv[:, :], AF.Silu)
    modT = sb("modT", [128, L, 24])
    s1 = sb("s1", [128, L, 8])
    def emit_mod(l):
        pb = xbank()
        for blk in range(6):
            w = load_w(D["W_mod"][l][:, blk * 512:(blk + 1) * 512], 8, 512)
            for mi in range(4):
                col = blk * 4 + mi
                for kt in range(8):
                    mm(pb[:, col:col + 1], w[:, kt, mi * 128:(mi + 1) * 128], cs_b[:, kt:kt + 1],
                       start=(kt == 0), stop=(kt == 7))
        tt(modT[:, l, :], pb[:, 0:24], bmod[:, l, :], ALU.add)
        stt(s1[:, l, :], modT[:, l, 8:16], 1.0, gnorm[:, l, :], ALU.add, ALU.mult)

    def mod_block_load(l, blk):
        return load_w(D["W_mod"][l][:, blk * 512:(blk + 1) * 512], 8, 512)

    def mod_block_compute(l, blk, w):
        pb = psB[:, 512:1024]
        for mi in range(4):
            for kt in range(8):
                mm(pb[:, 16 + mi:17 + mi], w[:, kt, mi * 128:(mi + 1) * 128], cs_b[:, kt:kt + 1],
                   start=(kt == 0), stop=(kt == 7))
        tt(modT[:, l, blk * 4:(blk + 1) * 4], pb[:, 16:20], bmod[:, l, blk * 4:(blk + 1) * 4], ALU.add)
        if blk == 5:
            stt(s1[:, l, :], modT[:, l, 8:16], 1.0, gnorm[:, l, :], ALU.add, ALU.mult)

    if nlayers > 0:
        emit_mod(0)

    def featmaj_proj(l, lo, n, consumer, w=None):
        if w is None:
            w = load_w(D["W_in"][l][:, lo:lo + n], 8, n)
        nm = (n + 127) // 128
        pending = None
        for ch in range(2):
            for mi in range(nm):
                m = min(128, n - mi * 128)
                pb = gbank()
                for kt in range(8):
                    mm(pb[0:m, :], w[:, kt, mi * 128:mi * 128 + m], hT[:, kt, ch * 512:(ch + 1) * 512],
                       start=(kt == 0), stop=(kt == 7))
                if pending is not None:
                    pending()
                pending = consumer(mi, ch, pb[0:m, :])
        if pending is not None:
            pending()

    def tokmaj_proj(l, lo, n, consumer, w=None):
        if w is None:
            w = load_w(D["W_in"][l][:, lo:lo + n], 8, n)
        pending = None
        for t_ in range(NT):
            pb = gbank()
            for kt in range(8):
                mm(pb[:, 0:n], hT[:, kt, t_ * 128:(t_ + 1) * 128], w[:, kt, :],
                   start=(kt == 0), stop=(kt == 7))
            if pending is not None:
                pending()
            pending = consumer(t_, pb[:, 0:n])
        if pending is not None:
            pending()

    ropet = [sb("ropet%d" % i, [128, 512]) for i in range(2)]

    ropeb = [ropet[i][:, :].bitcast(BF16) for i in range(2)]
    ropectr = [0]

    def rope(out, raw, npart, tok0, ntok, out_hi=None):
        pb = gbank()
        mm(pb[0:npart, 0:ntok], pswap_b[0:npart, 0:npart], raw, start=True, stop=True)
        rb_ = ropeb[ropectr[0] % 2]
        ropectr[0] += 1
        t1, t2 = rb_[:, 0:512], rb_[:, 512:1024]
        tt(t1[0:npart, 0:ntok], raw, cos2[0:npart, tok0:tok0 + ntok], ALU.mult)
        tt(t2[0:npart, 0:ntok], pb[0:npart, 0:ntok], sin2[0:npart, tok0:tok0 + ntok], ALU.mult)
        if out_hi is None:
            tt(out, t1[0:npart, 0:ntok], t2[0:npart, 0:ntok], ALU.add)
        else:
            tt(out[0:64, :], t1[0:64, 0:ntok], t2[0:64, 0:ntok], ALU.add)
            tt(out_hi, t1[64:128, 0:ntok], t2[64:128, 0:ntok], ALU.add)

    def out_part(l, g):
        wv_ = []
        for half in range(2):
            wv_.append(load_w(D["W_out"][l][g * 512:(g + 1) * 512, half * 512:(half + 1) * 512], 4, 512))
        for ch in range(2):
            for mo in range(8):
                w = wv_[mo // 4]
                pb = gbank()
                for kt in range(4):
                    mm(pb[:, :], w[:, kt, (mo % 4) * 128:(mo % 4 + 1) * 128], yT[:, kt, ch * 512:(ch + 1) * 512],
                       start=(kt == 0), stop=(kt == 3))
                xs = xT[:, mo, ch * 512:(ch + 1) * 512]
                stt(xs, pb[:, :], modT[:, l, 16 + mo:17 + mo], xs, ALU.mult, ALU.add)

    def attention(scores_fn, v_fn, scale, post_fn):
        iters = [(qs, kt) for qs in range(4) for kt in range(NKT)]

        def emit_scores(i):
            qs, kt = iters[i]
            pS = psA if i % 2 == 0 else psB
            for u in range(4):
                scores_fn(u, kt, qs, pS[:, u * 256:(u + 1) * 256], first=(u % 2 == 0))

        def emit_exp(i):
            qs, kt = iters[i]
            pS = psA if i % 2 == 0 else psB
            act(ptb[i % 2][:, :], pS[:, :], AF.Exp, bias=maskb[:, kt * 4 + qs:kt * 4 + qs + 1], scale=scale)

        deferred = []
        emit_scores(0)
        emit_exp(0)
        for i, (qs, kt) in enumerate(iters):
            PT = ptb[i % 2]
            if i + 1 < len(iters):
                emit_scores(i + 1)
            for u in range(4):
                mm(psO[:, u * 256:(u + 1) * 256], v_fn(u, kt), PT[:, u * 256:(u + 1) * 256],
                   start=(kt == 0 and u % 2 == 0), stop=(kt == NKT - 1), skip=True)
            for half in range(2):
                mm(psD[:, half * 512:(half + 1) * 512], ones_b[:, :], PT[:, half * 512:(half + 1) * 512],
                   start=(kt == 0), stop=(kt == NKT - 1))
            if i + 1 < len(iters):
                emit_exp(i + 1)
            while deferred and deferred[0][0] <= i:
                deferred.pop(0)[1]((psA if i % 2 == 0 else psB)[:, 0:512])
            if kt == NKT - 1:
                for half in range(2):
                    hs_ = slice(half * 512, (half + 1) * 512)
                    cp(osb[:, hs_], psO[:, hs_])
                    act(tmpA[:, hs_], psD[:, hs_], AF.Ln)

                def stage1(_pb, qs=qs, i=i):
                    act(tmpA[:, :], tmpA[:, :], AF.Exp, scale=-1.0)
                    tt(osb[:, :], osb[:, :], tmpA[:, :], ALU.mult)
                    st2 = post_fn(qs, osb)
                    if st2 is not None:
                        deferred.append((i + 4, st2))
                deferred.append((i + 2, stage1))
        while deferred:
            deferred.pop(0)[1](xbank())

    gx = sb("gx", [128, NT, 16])
    spf = sb("spf", [128, NT, 8])
    cs = sb("cs", [128, NT, 24])
    gsc = sb("gsc", [128, NT, 24])
    alast = sb("alast", [128, NT, 16])
    uu = sb("uu", [128, NT, 8])
    ktok = sh8[:, :].rearrange("p (t h d) -> p t h d", t=NT, h=4)
    vaug = attV[:, :, :].rearrange("p a b -> p (a b)")[:, 0:NT * 4 * 132].rearrange("p (t h d) -> p t h d", t=NT, h=4)
    vB = [sb("vB%d" % i, [128, 4, 132], BF16) for i in range(2)]
    Cst = sb("Cst", [128, 8, 130])
    Cbs = [sb("Cb%d" % i, [128, 8, 130], BF16) for i in range(2)]
    hsum = sb("hsum", [128, NT, 512], BF16)
    uraw = [sb("uraw%d" % i, [128, T + 2], BF16) for i in range(2)]
    cacc = tmpA
    for i in range(2):
        memset(uraw[i][:, 0:1], 0.0)
        memset(uraw[i][:, T + 1:T + 2], 0.0)
    ptm = [sb("ptm%d" % i, [128, 512], BF16) for i in range(2)]
    dtmp = sb("dtmp", [128, 2, 16])
    em0 = sb("em0", [128, 8])
    mch = sb("mch", [4, 2, 20])
    umax = sb("umax", [4, 2, 16])
    blast = sb("blast", [4, 2, 16])
    mout = sb("mout", [4, 8])
    mdiag = sb("mdiag", [4, 8, 4])
    emout = sb("emout", [128, 32])
    cout = sb("cout", [128, 1, 4, 130])
    tok_tmp = [sb("toktmp%d" % i, [128, 512]) for i in range(2)]
    tkc = [0]

    def toktmp():
        b = tok_tmp[tkc[0] % 2]
        tkc[0] += 1
        return b

    negw = sb("negw", [128, 2, 8])
    nlam = sb("nlam", [128, 4])
    gdl = sb("gdl", [128, 1])
    ssk = sb("ssk", [128, 8])
    ssm = sb("ssm", [128, 4])
    ctxf = sb("ctxf", [128, 2, 512])
    krraw = ddbuf[1].bitcast(BF16)[0:64, :]

    try:
      for l in range(nlayers):
          dma(gkv[:, 0, :], D["mla_kv_norm"][l:l + 1, :].partition_broadcast(128))
          dma(dlam[:, 0, :], D["diff_lambda"][l:l + 1, :].partition_broadcast(128))
          dma(mlnorm[:, 0, :], D["ml_norm"][l:l + 1, :].partition_broadcast(128))
          P.phase = "s1"
          for ch in range(2):
              pb = xbank()
              for kt in range(8):
                  sq = sqbuf()
                  act(sq[:, :], xT[:, kt, ch * 512:(ch + 1) * 512], AF.Square)
                  mm(pb[:, :], ones_b[:, :], sq[:, :], start=(kt == 0), stop=(kt == 7))
              act(rstd[:, ch * 512:(ch + 1) * 512], pb[:, :], AF.Ln, bias=eps_c[:, :], scale=1.0 / 1024)
              act(rstd[:, ch * 512:(ch + 1) * 512], rstd[:, ch * 512:(ch + 1) * 512], AF.Exp, scale=-0.5)
          for kt in range(8):
              stt(tmpA[:, :], xT[:, kt, :], s1[:, l, kt:kt + 1], rstd[:, :], ALU.mult, ALU.mult)
              act(hT[:, kt, :], tmpA[:, :], AF.Identity, bias=modT[:, l, kt:kt + 1], scale=1.0)

          if stop == "s1":
              raise _Stop()
          P.phase = "conv"
          ts(negw[:, 0, :], wconv[:, l, 0, :], kbar[:, 0:1], ALU.mult, -1.0, ALU.mult)
          ts(negw[:, 1, :], wconv[:, l, 2, :], kbar[:, 0:1], ALU.mult, -1.0, ALU.mult)
          def mg_consumer(t_, ps):
              tt(gx[:, t_, :], ps, gateb[:, l, :], ALU.add)
          tokmaj_proj(l, 5824, 16, mg_consumer)
          gx5 = gx[:, :, :].rearrange("p t (d i h) -> p t d i h", d=2, i=2)
          spv = spf[:, :, :].rearrange("p t (d h) -> p t d h", d=2)
          for d_ in range(2):
              act(spv[:, :, d_, :], gx5[:, :, d_, 1, :], AF.Exp, scale=-1.0)
          act(spf[:, :, :], spf[:, :, :], AF.Ln, bias=one_c[:, :], scale=1.0)
          def gates_part_b():
              pb = xbank()
              for t_ in range(NT):
                  c0 = t_ * 24
                  mm(pb[:, c0:c0 + 4], maskF_f, spf[:, t_, 0:4], start=True, stop=True)
                  mm(pb[:, c0 + 4:c0 + 8], maskB_f, spf[:, t_, 4:8], start=True, stop=True)
                  mm(pb[:, c0 + 8:c0 + 16], selA_f, spf[:, t_, :], start=True, stop=True)
                  mm(pb[:, c0 + 16:c0 + 24], selB_f, spf[:, t_, :], start=True, stop=True)
              cp(cs[:, :, :], pb[:, 0:NT * 24].rearrange("p (t c) -> p t c", c=24))
              uuv = uu[:, :, :].rearrange("p t (d h) -> p t d h", d=2)
              csv = cs[:, :, 0:8].rearrange("p t (d h) -> p t d h", d=2)
              for d_ in range(2):
                  tt(uuv[:, :, d_, :], gx5[:, :, d_, 0, :], csv[:, :, d_, :], ALU.add)
              act(gsc[:, :, 0:8], uu[:, :, :], AF.Exp)
              act(gsc[:, :, 8:16], cs[:, :, 0:8], AF.Exp, bias=lnq_c[:, :], scale=-1.0)
              act(gsc[:, :, 16:24], cs[:, :, 0:8], AF.Exp, bias=nlnq_c[:, :], scale=1.0)
              act(alast[:, :, :], cs[:, :, 8:24], AF.Exp, scale=-1.0)


          for (lo, dst, ncap) in ((3264, attQ, T), (3776, attK, TK)):
              w = load_w(D["W_in"][l][:, lo:lo + 512], 8, 512)
              for mi in range(4):
                  ur = uraw[mi % 2]
                  cacc = tmpA if mi % 2 == 0 else tmpB
                  for ch in range(2):
                      pb = gbank()
                      for kt in range(8):
                          mm(pb[:, :], w[:, kt, mi * 128:(mi + 1) * 128], hT[:, kt, ch * 512:(ch + 1) * 512],
                             start=(kt == 0), stop=(kt == 7))
                      cp(ur[:, 1 + ch * 512:1 + (ch + 1) * 512], pb[:, :], eng="act")
                  ct = (lo - 3264) // 128 + mi
                  ts(cacc[:, :], ur[:, 1:T + 1], wconv[:, l, 1, ct:ct + 1], ALU.mult)
                  stt(cacc[:, :], ur[:, 0:T], wconv[:, l, 0, ct:ct + 1], cacc[:, :], ALU.mult, ALU.add)
                  stt(cacc[:, :], ur[:, 2:T + 2], wconv[:, l, 2, ct:ct + 1], cacc[:, :], ALU.mult, ALU.add)
                  cview = cacc[:, :].rearrange("p (a b) -> p a b", b=256)
                  uview = ur[:, 0:T].rearrange("p (a b) -> p a b", b=256)
                  uview2 = ur[:, 2:T + 2].rearrange("p (a b) -> p a b", b=256)
                  stt(cview[:, 1:4, 0], uview[:, 1:4, 0], negw[:, 0, ct:ct + 1], cview[:, 1:4, 0], ALU.mult, ALU.add)
                  stt(cview[:, 0:3, 255], uview2[:, 0:3, 255], negw[:, 1, ct:ct + 1], cview[:, 0:3, 255],
                      ALU.mult, ALU.add)
                  act(dst[:, mi, 0:T], cacc[:, :], AF.Silu)
              if lo == 3264:
                  gates_part_b()
          if stop == "conv":
              raise _Stop()
          qT, kT = attQ, attK

          P.phase = "mvgates"
          memset(vaug[0:64, :, :, 128:129], 1.0)
          memset(vaug[64:128, :, :, 128:129], 0.0)
          memset(vaug[0:64, :, :, 129:130], 0.0)
          memset(vaug[64:128, :, :, 129:130], 1.0)
          memset(vaug[:, :, :, 130:132], 1.0)
          def mv_consumer(t_, ps):
              cp(vaug[:, t_, :, 0:128], ps.rearrange("p (h d) -> p h d", h=4), eng="act")
          tokmaj_proj(l, 4288, 512, mv_consumer)

          if stop == "gates":
              raise _Stop()
          for t_ in range(NT):
              pbb = gbank().bitcast(BF16)
              for h in range(4):
                  tr(pbb[:, h * 128:(h + 1) * 128], kT[:, h, t_ * 128:(t_ + 1) * 128], ident_b)
              cp(ktok[:, t_, :, :], pbb[:, 0:512].rearrange("p (h d) -> p h d", h=4))

          for d_ in range(2):
              pu = psO
              for t_ in range(NT):
                  tr(pu[0:4, t_ * 128:(t_ + 1) * 128], uu[:, t_, d_ * 4:(d_ + 1) * 4], ident_f)
              red(umax[:, d_, :], pu[0:4, :].rearrange("p (c s) -> p c s", s=64), ALU.max)
              pv = psD
              for t_ in range(NT):
                  tr(pv[0:4, t_ * 128:(t_ + 1) * 128], spf[:, t_, d_ * 4:(d_ + 1) * 4], ident_f)
              red(blast[:, d_, :], pv[0:4, :].rearrange("p (c s) -> p c s", s=64), ALU.add)

          dma(em0[:, :], D["sm"][l:l + 1, :].partition_broadcast(128))
          for d_ in range(2):
              dma(mch[:, d_, 0:1], D["sm"][l, d_ * 4:(d_ + 1) * 4].rearrange("(p o) -> p o", o=1))
          act(em0[:, :], em0[:, :], AF.Exp)
          for d_ in range(2):
              for h in range(4):
                  dma(Cst[:, d_ * 4 + h, 0:128], D["sC"][l, d_, h])
                  dma(Cst[:, d_ * 4 + h, 128:129], D["sn"][l, d_, h].rearrange("(p o) -> p o", o=1))
          memset(Cst[:, :, 129:130], 0.0)
          for i in range(8):
              ts(Cst[:, i, :], Cst[:, i, :], em0[:, i:i + 1], ALU.mult)
          cp(Cbs[0][:, :, :], Cst[:, :, :])
          cp(Cbs[1][:, :, :], Cst[:, :, :])

          if stop == "prescan":
              raise _Stop()
          P.phase = "scan"
          for d_ in range(2):
              order = list(range(NT)) if d_ == 0 else list(range(NT - 1, -1, -1))
              mcol = 0
              cbi = 0
              C4 = Cst[:, d_ * 4:(d_ + 1) * 4, 0:129]
              def emit_vb(step_):
                  tq = order[step_]
                  Bbc_ = gsc[:, tq, d_ * 4:d_ * 4 + 4].unsqueeze(2).to_broadcast([128, 4, 132])
                  tt(vB[step_ % 2][:, :, :], vaug[:, tq, :, :], Bbc_, ALU.mult, eng="pool")

              emit_vb(0)
              modw = {}
              for step, t_ in enumerate(order):
                  if step + 1 < NT:
                      emit_vb(step + 1)
                  if d_ == 0 and l + 1 < nlayers and step < 6:
                      modw[step] = mod_block_load(l + 1, step)
                  if d_ == 0 and l + 1 < nlayers and 1 <= step < 7:
                      mod_block_compute(l + 1, step - 1, modw.pop(step - 1))
                  if d_ == 1 and step == 5:
                      w_zc = load_w(D["W_in"][l][:, 5312:5824], 8, 512)
                  if d_ == 1 and step == 6:
                      w_mo = load_w(D["W_in"][l][:, 4800:5312], 8, 512)
                  pH = (psO if step % 2 == 0 else psA)[:, 0:512]
                  pSm = (psO if step % 2 == 0 else psA)[:, 512:1024]
                  pH4 = pH.rearrange("p (h c) -> p h c", h=4)
                  tsl = slice(t_ * 128, (t_ + 1) * 128)
                  vb = vB[step % 2]
                  pS = psB[:, 0:512]
                  for h in range(4):
                      mm(pS[:, h * 128:(h + 1) * 128], kT[:, h, tsl], qT[:, h, tsl], start=True, stop=True)
                  pm = ptm[step % 2]
                  tt(pm[:, :], pS[:, :], mask4[:, d_, :, :].rearrange("p h t -> p (h t)"), ALU.mult)
                  halves = (0, 1) if d_ == 0 else (1, 0)
                  pCs = {}
                  for hf in halves:
                      rs = slice(hf * 64, hf * 64 + 64)
                      pC = psD[:, hf * 512:(hf + 1) * 512]
                      pCs[hf] = pC
                      for h in range(4):
                          mm(pC[:, h * 128:(h + 1) * 128], ktok[rs, t_, h, :], vb[rs, h, 0:128], start=True, stop=True)
                  for h in range(4):
                      mm(pH4[:, h, :], pm[:, h * 128:(h + 1) * 128], vb[:, h, 0:128],
                         start=(h == 0), stop=False, skip=True)
                  for h in range(4):
                      mm(pSm[:, h:h + 1], pm[:, h * 128:(h + 1) * 128], vb[:, h, 130:131],
                         start=(h == 0), stop=False, skip=True)
                  for h in range(4):
                      mm(pSm[:, 8 + 2 * h:10 + 2 * h], ktok[:, t_, h, :], vb[:, h, 128:130],
                         start=False, stop=True, skip=True)
                  for hf in halves:
                      rs = slice(hf * 64, hf * 64 + 64)
                      qs_ = slice(t_ * 128 + hf * 64, t_ * 128 + hf * 64 + 64)
                      Cb = Cbs[cbi]
                      for h in range(4):
                          mm(pH4[rs, h, :], qT[:, h, qs_], Cb[:, d_ * 4 + h, 0:128],
                             start=False, stop=True, skip=True)
                      for h in range(4):
                          mm(pSm[rs, h:h + 1], qT[:, h, qs_], Cb[:, d_ * 4 + h, 128:129],
                             start=False, stop=True, skip=True)
                      a4 = alast[:, t_, hf * 8 + d_ * 4:hf * 8 + d_ * 4 + 4]
                      abc = a4.unsqueeze(2).to_broadcast([128, 4, 128])
                      pC4 = pCs[hf].rearrange("p (h c) -> p h c", h=4)
                      tt(C4[:, :, 0:128], pC4, C4[:, :, 0:128], ALU.add)
                      tt(C4[:, :, 0:128], C4[:, :, 0:128], abc, ALU.mult)
                      dn4 = pSm[:, 8:16].rearrange("p (h two) -> p h two", two=2)[:, :, hf]
                      tt(C4[:, :, 128], dn4, C4[:, :, 128], ALU.add)
                      tt(C4[:, :, 128], C4[:, :, 128], a4, ALU.mult)
                      c_orig = t_ * 2 + hf
                      cidx = c_orig if d_ == 0 else 15 - c_orig
                      prev = mch[:, d_, mcol:mcol + 1]
                      nxt = mch[:, d_, mcol + 1:mcol + 2]
                      tt(nxt, prev, umax[:, d_, c_orig:c_orig + 1], ALU.max)
                      tt(nxt, nxt, blast[:, d_, c_orig:c_orig + 1], ALU.subtract)
                      mcol += 1
                      if cidx % 4 == 3:
                          seg = c_orig // 4
                          cp(mout[:, seg * 2 + d_:seg * 2 + d_ + 1], nxt)
                          ts(mdiag[:, seg * 2 + d_, :], ident_f[0:4, 0:4], nxt, ALU.mult)
                          pe_ = psB[:, 512:1024]
                          mm(pe_[:, 0:4], ones_f[0:4, :], mdiag[:, seg * 2 + d_, :], start=True, stop=True)
                          ec = (seg * 2 + d_) * 4
                          act(emout[:, ec:ec + 4], pe_[:, 0:4], AF.Exp, scale=-1.0)
                          ebc = emout[:, ec:ec + 4].unsqueeze(2).to_broadcast([128, 4, 129])
                          tt(cout[:, 0, :, 0:129], C4, ebc, ALU.mult)
                          dma(D["o_C"][l, seg, d_].rearrange("h k v -> k h v"), cout[:, 0, :, 0:128])
                          dma(D["o_n"][l, seg, d_].rearrange("h (k o) -> k h o", o=1), cout[:, 0, :, 128:129])
                          if cidx != 15:
                              nseg = seg + 1 if d_ == 0 else seg - 1
                              kbc = keep[:, nseg:nseg + 1].unsqueeze(2).to_broadcast([128, 4, 129])
                              tt(C4, C4, kbc, ALU.mult)
                              nn = mch[:, d_, mcol + 1:mcol + 2]
                              tt(nn, nxt, keep[0:4, nseg:nseg + 1], ALU.mult)
                              mcol += 1
                      cbi = 1 - cbi
                      cp(Cbs[cbi][:, d_ * 4:(d_ + 1) * 4, 0:129], C4)
                  invE = gsc[:, t_, 16 + d_ * 4:16 + d_ * 4 + 4]
                  dt_ = dtmp[:, step % 2, :]
                  act(dt_[:, 0:4], pSm[:, 0:4], AF.Abs)
                  tt(dt_[:, 4:8], dt_[:, 0:4], invE, ALU.max)
                  recip(dt_[:, 12:16], dt_[:, 4:8])
                  h4 = hsum[:, t_, :].rearrange("p (h d) -> p h d", h=4)
                  if d_ == 0:
                      for h in range(4):
                          act(h4[:, h, :], pH4[:, h, :], AF.Copy, scale=dt_[:, 12 + h:13 + h])
                  else:
                      hb = ropet[step % 2][:, :].rearrange("p (h d) -> p h d", h=4)
                      for h in range(4):
                          act(hb[:, h, :], pH4[:, h, :], AF.Copy, scale=dt_[:, 12 + h:13 + h])
                      tt(h4, hb, h4, ALU.add, eng="pool")
          dma(D["o_m"][l].rearrange("s d (h o) -> h s d o", o=1), mout[:, :].rearrange("h (s d o) -> h s d o", d=2, o=1))

          if stop == "scan":
              raise _Stop()
          P.phase = "mlstm_out"
          def z_consumer(mi, ch, ps):
              act(zT[:, mi, ch * 512:(ch + 1) * 512], ps, AF.Silu)
          featmaj_proj(l, 5312, 512, z_consumer, w=w_zc)

          def mo_consumer(t_, ps):
              sg = toktmp()
              act(sg[:, :], ps, AF.Sigmoid)
              tt(sg[:, :], sg[:, :], hsum[:, t_, :], ALU.mult)
              junk = tmpB
              for h in range(4):
                  act(junk[:, 0:128], sg[:, h * 128:(h + 1) * 128], AF.Square, accum=ssm[:, h:h + 1])
              act(ssm[:, :], ssm[:, :], AF.Ln, bias=eps_c[:, :], scale=1.0 / 128)
              act(ssm[:, :], ssm[:, :], AF.Exp, scale=-0.5)
              ob = sqbuf()
              for h in range(4):
                  stt(ob[:, h * 128:(h + 1) * 128], sg[:, h * 128:(h + 1) * 128], ssm[:, h:h + 1],
                      mlnorm[:, 0, h * 128:(h + 1) * 128], ALU.mult, ALU.mult)
              def tail():
                  pbb = gbank().bitcast(BF16)
                  for h in range(4):
                      tr(pbb[:, h * 128:(h + 1) * 128], ob[:, h * 128:(h + 1) * 128], ident_b)
                  tt(yT[:, :, t_ * 128:(t_ + 1) * 128], pbb[:, 0:512].rearrange("p (h t) -> p h t", h=4),
                     zT[:, :, t_ * 128:(t_ + 1) * 128], ALU.mult)
              return tail
          tokmaj_proj(l, 4800, 512, mo_consumer, w=w_mo)
          out_part(l, 2)

          if stop == "mlstm":
              raise _Stop()
          P.phase = "mla_proj"
          memset(sh8[64:128, :], 0.0, eng="dve")
          pq = [None]

          def cq_consumer(mi, ch, ps):
              if mi == 0:
                  pq[0] = xbank()
              ts(tmpcq[:, mi, ch * 512:(ch + 1) * 512], ps, gq[:, l, mi:mi + 1], ALU.mult)
              sq = sqbuf()
              act(sq[:, :], ps, AF.Square)
              pqb = pq[0]

              def tail():
                  mm(pqb[:, :], ones_b[:, :], sq[:, :], start=(mi == 0), stop=(mi == 2))
                  if mi == 2:
                      act(rstd[:, ch * 512:(ch + 1) * 512], pqb[:, :], AF.Ln, bias=eps_c[:, :], scale=1.0 / 384)
                      act(rstd[:, ch * 512:(ch + 1) * 512], rstd[:, ch * 512:(ch + 1) * 512], AF.Exp, scale=-0.5)
              return tail
          tmpcq = hsum[:, :, :].rearrange("p a b -> p (a b)")[:, 0:3 * T].rearrange("p (a b) -> p a b", a=3)
          featmaj_proj(l, 0, 384, cq_consumer)
          wuq = load_w(D["W_uq"][l], 3, 768)
          for ch in range(2):
              csl = slice(ch * 512, (ch + 1) * 512)
              for h in range(4):
                  pb = gbank()
                  for kt in range(3):
                      mm(pb[:, :], wuq[:, kt, h * 192:h * 192 + 128], tmpcq[:, kt, csl], start=(kt == 0), stop=(kt == 2))
                  tt(attQ[:, h, csl], pb[:, :], rstd[:, csl], ALU.mult)
                  pb = gbank()
                  for kt in range(3):
                      mm(pb[0:64, :], wuq[:, kt, h * 192 + 128:h * 192 + 192], tmpcq[:, kt, csl],
                         start=(kt == 0), stop=(kt == 2))
                  tt(qropeT[:, h, csl], pb[0:64, :], rstd[0:64, csl], ALU.mult)

          for ch in range(2):
              for h in range(4):
                  csl = slice(ch * 512, (ch + 1) * 512)
                  rope(qropeT[:, h, csl], qropeT[:, h, csl], 64, ch * 512, 512)
          def kv_consumer(t_, ps):
              tsl = slice(t_ * 128, (t_ + 1) * 128)
              junk = tmpB
              act(junk[:, 0:256], ps[:, 0:256], AF.Square, accum=ssk[:, t_:t_ + 1])
              act(ssk[:, t_:t_ + 1], ssk[:, t_:t_ + 1], AF.Ln, bias=eps_c[:, :], scale=1.0 / 256)
              act(ssk[:, t_:t_ + 1], ssk[:, t_:t_ + 1], AF.Exp, scale=-0.5)
              tk = toktmp()
              stt(tk[:, 0:256], ps[:, 0:256], ssk[:, t_:t_ + 1], gkv[:, 0, :], ALU.mult, ALU.mult)
              cp(tk[:, 256:320], ps[:, 256:320], eng="act")
              dma(D["o_ckv"][l, tsl, :], tk[:, 0:256])
              dma(D["o_krope"][l, tsl, :], tk[:, 256:320])
              def tail():
                  pb = gbank()
                  for kt in range(2):
                      tr(pb[:, kt * 128:(kt + 1) * 128], tk[:, kt * 128:(kt + 1) * 128], ident_f)
                  tr(pb[0:64, 256:384], tk[:, 256:320], ident_f)
                  cp(ckvnT[:, :, tsl], pb[:, 0:256].rearrange("p (k t) -> p k t", k=2))
                  cp(krraw[0:64, tsl], pb[0:64, 256:384], eng="act")
              return tail
          tokmaj_proj(l, 384, 320, kv_consumer)
          for ch in range(2):
              rope(kropeT[:, ch * 512:(ch + 1) * 512], krraw[0:64, ch * 512:(ch + 1) * 512], 64, ch * 512, 512)
          dma(ctxf[:, :, 0:256], D["cckv"][l].rearrange("(t p) c -> p t c", p=128))
          dma(ctxf[:, :, 256:320], D["ckrope"][l].rearrange("(t p) c -> p t c", p=128))
          for t2 in range(2):
              pb = gbank()
              for kt in range(2):
                  tr(pb[:, kt * 128:(kt + 1) * 128], ctxf[:, t2, kt * 128:(kt + 1) * 128], ident_f)
              tr(pb[0:64, 256:384], ctxf[:, t2, 256:320], ident_f)
              ksl = slice(T + t2 * 128, T + (t2 + 1) * 128)
              cp(ckvnT[:, :, ksl], pb[:, 0:256].rearrange("p (k t) -> p k t", k=2))
              cp(kropeT[:, ksl], pb[0:64, 256:384], eng="act")
          wkv = load_w(D["W_ukv"][l], 2, 1024)
          wkv5 = wkv.rearrange("p k (h w d) -> p k h w d", h=4, w=2)
          for h in range(4):
              for (k0, kn) in ((0, 512), (512, 512), (1024, 256)):
                  pb = gbank()
                  for kt in range(2):
                      mm(pb[:, 0:kn], wkv5[:, kt, h, 0, :], ckvnT[:, kt, k0:k0 + kn], start=(kt == 0), stop=(kt == 1))
                  cp(attK[:, h, k0:k0 + kn], pb[:, 0:kn], eng="act")
          for kt_ in range(NKT):
              pb = gbank()
              for kt in range(2):
                  mm(pb[:, :].rearrange("p (h d) -> p h d", h=4), ckvnT[:, kt, kt_ * 128:(kt_ + 1) * 128],
                     wkv5[:, kt, :, 1, :], start=(kt == 0), stop=(kt == 1))
              cp(attV[:, kt_, :], pb[:, :])
          featmaj_proj(l, 704, 512, z_consumer)

          P.phase = "mla_att"
          def mla_scores(u, kt, qs, out, first):
              mm(out, attK[:, u, kt * 128:(kt + 1) * 128], attQ[:, u, qs * 256:(qs + 1) * 256],
                 start=first, stop=False, skip=True)
              mm(out, kropeT_full[:, kt * 128:(kt + 1) * 128], qrope128[:, u, qs * 256:(qs + 1) * 256],
                 start=False, stop=True, skip=True)

          def mla_post(qs, o):
              tt(yT[:, :, qs * 256:(qs + 1) * 256], o[:, :].rearrange("p (u q) -> p u q", u=4),
                 zT[:, :, qs * 256:(qs + 1) * 256], ALU.mult)
          attention(mla_scores, lambda u, kt: attV[:, kt, u * 128:(u + 1) * 128], MLA_SCALE, mla_post)
          out_part(l, 0)

          if stop == "mla":
              raise _Stop()
          P.phase = "diff_proj"
          lam_init = 0.8 - 0.6 * math.exp(-0.3 * l)
          junk = tmpB
          stt(junk[:, 0:64], dlam[:, 0, 0:64], 1.0, dlam[:, 0, 64:128], ALU.mult, ALU.mult)
          red(nlam[:, 0:1], junk[:, 0:64], ALU.add)
          stt(junk[:, 64:128], dlam[:, 0, 128:192], 1.0, dlam[:, 0, 192:256], ALU.mult, ALU.mult)
          red(nlam[:, 1:2], junk[:, 64:128], ALU.add)
          act(nlam[:, 0:2], nlam[:, 0:2], AF.Exp)
          tt(nlam[:, 2:3], nlam[:, 1:2], nlam[:, 0:1], ALU.subtract)
          ts(nlam[:, 3:4], nlam[:, 2:3], -lam_init, ALU.add)
          ts(gdl[:, :], dnorm[:, l:l + 1], 1.0 - lam_init, ALU.mult)

          dma(ctxf[:, :, :], D["cdk"][l].rearrange("(t p) c -> p t c", p=128))
          q1v = sh8[:, :].rearrange("p (a b) -> p a b", a=4)
          memset(sh8[0:64, :], 0.0, eng="dve")

          def dq_consumer(mi, ch, ps):
              cp(attQ[:, mi, ch * 512:(ch + 1) * 512], ps, eng="act")
          featmaj_proj(l, 1216, 512, dq_consumer)

          def dk_consumer(mi, ch, ps):
              cp(attK[:, mi, ch * 512:(ch + 1) * 512], ps, eng="act")
          featmaj_proj(l, 1728, 512, dk_consumer)
          for ch in range(2):
              for mi in range(4):
                  csl = slice(ch * 512, (ch + 1) * 512)
                  rope(attQ[:, mi, csl], attQ[:, mi, csl], 128, ch * 512, 512, out_hi=q1v[64:128, mi, csl])
                  rope(attK[:, mi, csl], attK[:, mi, csl], 128, ch * 512, 512)
          memset(attQ[64:128, :, :], 0.0, eng="dve")

          def dk_tok_consumer(t_, ps):
              tk = toktmp()
              cp(tk[:, :], ps, eng="act")
              dma(D["o_dk"][l, t_ * 128:(t_ + 1) * 128, :], tk[:, :])
          tokmaj_proj(l, 1728, 512, dk_tok_consumer)

          def dv_consumer(t_, ps):
              tk = toktmp()
              cp(tk[:, :], ps, eng="act")
              dma(D["o_dv"][l, t_ * 128:(t_ + 1) * 128, :], tk[:, :])
              cp(attV[:, t_, :], ps)
          tokmaj_proj(l, 2240, 512, dv_consumer)
          dma(attV[:, 8:10, :], D["cdv"][l].rearrange("(t p) c -> p t c", p=128), eng="pool")
          for t2 in range(2):
              pb = gbank()
              for h in range(4):
                  tr(pb[:, h * 128:(h + 1) * 128], ctxf[:, t2, h * 128:(h + 1) * 128], ident_f)
              cp(attK[:, :, T + t2 * 128:T + (t2 + 1) * 128], pb[:, :].rearrange("p (h t) -> p h t", h=4))
          featmaj_proj(l, 2752, 512, z_consumer)
          P.phase = "diff_att"
          for pair in range(2):
              def d_scores(u, kt, qs, out, first, pair=pair):
                  h = pair * 2 + u % 2
                  m_ = u // 2
                  qsrc = attQ if m_ == 0 else q1v
                  mm(out, attK[:, h, kt * 128:(kt + 1) * 128],
                     qsrc[:, h, qs * 256:(qs + 1) * 256], start=first, stop=True, skip=True)

              def d_post(qs, o, pair=pair):
                  o4 = o[:, :].rearrange("p (m h q) -> p m h q", h=2, m=2)
                  ddb = ddbuf[qs % 2]
                  dd = ddb[:, :].rearrange("p (h q) -> p h q", h=2)
                  stt(dd, o4[:, 1, :, :], nlam[:, 3:4], o4[:, 0, :, :], ALU.mult, ALU.add)
                  sq = sqbuf()
                  tt(sq[:, :], ddb[:, :], ddb[:, :], ALU.mult)

                  def stage2(pb):
                      mm(pb[:, :], ones_b[:, :], sq[:, :], start=True, stop=True)
                      rs_ = tmpB[:, 512:1024]
                      act(rs_, pb[:, :], AF.Ln, bias=eps_c[:, :], scale=1.0 / 128)
                      act(rs_, rs_, AF.Exp, scale=-0.5)
                      stt(rs_, ddb[:, :], gdl[:, 0:1], rs_, ALU.mult, ALU.mult)
                      tt(yT[:, pair * 2:pair * 2 + 2, qs * 256:(qs + 1) * 256], rs_.rearrange("p (h q) -> p h q", h=2),
                         zT[:, pair * 2:pair * 2 + 2, qs * 256:(qs + 1) * 256], ALU.mult)
                  return stage2
              attention(d_scores, lambda u, kt, pair=pair: attV[:, kt, (pair * 2 + u % 2) * 128:(pair * 2 + u % 2 + 1) * 128],
                        DIFF_SCALE, d_post)
          out_part(l, 1)

    except _Stop:
        pass
    P.phase = "final"
    for ch in range(2):
        pb = xbank()
        for kt in range(8):
            sq = sqbuf()
            act(sq[:, :], xT[:, kt, ch * 512:(ch + 1) * 512], AF.Square)
            mm(pb[:, :], ones_b[:, :], sq[:, :], start=(kt == 0), stop=(kt == 7))
        act(rstd[:, ch * 512:(ch + 1) * 512], pb[:, :], AF.Ln, bias=eps_c[:, :], scale=1.0 / 1024)
        act(rstd[:, ch * 512:(ch + 1) * 512], rstd[:, ch * 512:(ch + 1) * 512], AF.Exp, scale=-0.5)
    for kt in range(8):
        stt(xT[:, kt, :], xT[:, kt, :], gfin[:, kt:kt + 1], rstd[:, :], ALU.mult, ALU.mult)
    for t_ in range(NT):
        xo = xin[t_ % 2]
        for half in range(2):
            pb = gbank()
            for j in range(4):
                kt = half * 4 + j
                tr(pb[:, j * 128:(j + 1) * 128], xT[:, kt, t_ * 128:(t_ + 1) * 128], ident_f)
            cp(xo[:, half * 512:(half + 1) * 512], pb[:, :], eng="act" if half else "dve")
        dma(D["y"][t_ * 128:(t_ + 1) * 128, :], xo[:, :])
    P.emit()
    global _LASTP
    _LASTP = P
    import os
    if os.environ.get("KDBG"):
        import json
        json.dump([(o.eng, ph) for o, ph in zip(P.ops, P.phases)], open(os.environ["KDBG"], "w"))
    return nc


def _consts():
    c = np.zeros((128, 6, 128), np.float32)
    i = np.arange(128)
    c[:, 0, :] = np.eye(128)
    part = (i + 32) % 64 + 64 * (i // 64)
    c[i, 1, part] = 1.0
    same = (i[:, None] // 64) == (i[None, :] // 64)
    c[:, 2, :] = (same & (i[:, None] <= i[None, :])).astype(np.float32)
    c[:, 3, :] = (same & (i[:, None] >= i[None, :])).astype(np.float32)
    c[:, 4, :] = (i[:, None] < 64).astype(np.float32) * np.ones((1, 128), np.float32)
    c[:, 5, :] = (i[:, None] >= 64).astype(np.float32) * np.ones((1, 128), np.float32)
    return c


def _rope_tables(identity):
    if identity:
        return np.ones((128, T), np.float32), np.zeros((128, T), np.float32)
    n_freq = 16
    inv = 10000.0 ** (-np.arange(n_freq, dtype=np.float64) / n_freq)
    t = np.arange(T)
    row = (t // 64).astype(np.float64)
    col = (t % 64).astype(np.float64)
    ang = np.concatenate([row[:, None] * inv, col[:, None] * inv], axis=-1)
    c = np.cos(ang).T.astype(np.float32)
    s = np.sin(ang).T.astype(np.float32)
    cos2 = np.concatenate([c, c, c, c], 0)
    sin2 = np.concatenate([-s, s, -s, s], 0)
    return np.ascontiguousarray(cos2), np.ascontiguousarray(sin2)


_NC_CACHE = {}


def kernel(x_prompt, x_sample, cache_mla_ckv, cache_mla_krope, cache_diff_k, cache_diff_v,
           state_mlstm_C, state_mlstm_n, state_mlstm_m, c, c_ctx, g_norm, W_mod, b_mod, W_in,
           mla_q_norm, W_uq, mla_kv_norm, W_ukv, diff_lambda, diff_norm, ml_conv, ml_gate_b,
           ml_norm, W_out, g_final):
    f = lambda a: np.ascontiguousarray(np.asarray(a, dtype=np.float32))
    shared = {"g_norm": f(g_norm), "W_mod": f(W_mod), "b_mod": f(b_mod), "W_in": f(W_in),
              "mla_q_norm": f(mla_q_norm), "W_uq": f(W_uq), "mla_kv_norm": f(mla_kv_norm),
              "W_ukv": f(W_ukv), "diff_lambda": f(diff_lambda).reshape(L, 256), "diff_norm": f(diff_norm),
              "ml_conv": f(ml_conv), "ml_gate_b": f(ml_gate_b).reshape(L, 16),
              "ml_norm": f(ml_norm).reshape(L, 512), "W_out": f(W_out), "g_final": f(g_final),
              "consts": _consts()}
    x_prompt, x_sample = f(x_prompt), f(x_sample)
    in_maps = []
    for core in range(8):
        m = dict(shared)
        if core < 4:
            b = core
            m["x"] = x_sample[b]
            m["cvec"] = f(c)[b]
            m["cckv"] = f(cache_mla_ckv)[b]
            m["ckrope"] = f(cache_mla_krope)[b]
            m["cdk"] = f(cache_diff_k)[b].reshape(L, 256, 512)
            m["cdv"] = f(cache_diff_v)[b].reshape(L, 256, 512)
            m["sC"] = f(state_mlstm_C)[b]
            m["sn"] = f(state_mlstm_n)[b]
            m["sm"] = f(state_mlstm_m)[b].reshape(L, 8)
            m["maskb"] = np.zeros((128, 40), np.float32)
            m["keep"] = np.ones((128, 4), np.float32)
            m["kbar"] = np.zeros((128, 1), np.float32)
            m["cos2"], m["sin2"] = _rope_tables(False)
        else:
            b0 = (core - 4) * 4
            m["x"] = np.ascontiguousarray(x_prompt[b0:b0 + 4].reshape(T, 1024))
            m["cvec"] = f(c_ctx)
            m["cckv"] = np.zeros((L, 256, 256), np.float32)
            m["ckrope"] = np.zeros((L, 256, 64), np.float32)
            m["cdk"] = np.zeros((L, 256, 512), np.float32)
            m["cdv"] = np.zeros((L, 256, 512), np.float32)
            m["sC"] = np.zeros((L, 2, 4, 128, 128), np.float32)
            m["sn"] = np.zeros((L, 2, 4, 128), np.float32)
            m["sm"] = np.zeros((L, 8), np.float32)
            mb = np.full((128, NKT, 4), NEG, np.float32)
            for kt in range(8):
                mb[:, kt, kt // 2] = 0.0
            m["maskb"] = mb.reshape(128, 40)
            m["keep"] = np.zeros((128, 4), np.float32)
            m["kbar"] = np.ones((128, 1), np.float32)
            m["cos2"], m["sin2"] = _rope_tables(True)
        in_maps.append({k: np.ascontiguousarray(v) for k, v in m.items()})
    if "nc" not in _NC_CACHE:
        _NC_CACHE["nc"] = build()
    res = run_bass_kernel_spmd(_NC_CACHE["nc"], in_maps, core_ids=list(range(8)))
    R = res.results
    y_sample = np.stack([R[b]["y"] for b in range(4)], 0).astype(np.float32)
    y_prompt = np.zeros((16, 256, 1024), np.float32)
    new_ckv = np.zeros((16, L, 256, 256), np.float32)
    new_krope = np.zeros((16, L, 256, 64), np.float32)
    new_dk = np.zeros((16, L, 256, 4, 128), np.float32)
    new_dv = np.zeros((16, L, 256, 4, 128), np.float32)
    new_C = np.zeros((16, L, 2, 4, 128, 128), np.float32)
    new_n = np.zeros((16, L, 2, 4, 128), np.float32)
    new_m = np.zeros((16, L, 2, 4), np.float32)
    for core in range(4, 8):
        r = R[core]
        for s in range(4):
            b = (core - 4) * 4 + s
            sl = slice(s * 256, (s + 1) * 256)
            y_prompt[b] = r["y"][sl]
            new_ckv[b] = r["o_ckv"][:, sl, :]
            new_krope[b] = r["o_krope"][:, sl, :]
            new_dk[b] = r["o_dk"][:, sl, :].reshape(L, 256, 4, 128)
            new_dv[b] = r["o_dv"][:, sl, :].reshape(L, 256, 4, 128)
            new_C[b] = r["o_C"][:, s]
            new_n[b] = r["o_n"][:, s]
            new_m[b] = r["o_m"][:, s]
    return (y_prompt, y_sample, new_ckv, new_krope, new_dk, new_dv, new_C, new_n, new_m)
```
